# Optimizing a Trainium2 kernel written in Bass

```python
import math
import jax, jax.numpy as jnp
from jax import lax
import numpy as np

D_MODEL = 1024
BATCH = 8
SEQ = 2048
DEPTH = 1

MEM_TOKENS = 256
MLSTM_WIDTH = D_MODEL
MLSTM_HEADS = 4
MLSTM_HEAD_DIM = MLSTM_WIDTH // MLSTM_HEADS
MLSTM_CONV = 4
MLSTM_QKV_BLOCK = 4
MLSTM_CHUNK = 64
MOBA_WIDTH = D_MODEL // 2
MOBA_HEADS = 4
MOBA_HEAD_DIM = MOBA_WIDTH // MOBA_HEADS
MOBA_BLOCK = 256
MOBA_TOP_K = 3
MOBA_QUERY_GROUP = 32
MEM_WIDTH = D_MODEL // 2
MEM_HEADS = 4
MEM_HEAD_DIM = MEM_WIDTH // MEM_HEADS

MIX_WIDTH = MLSTM_WIDTH + MOBA_WIDTH + MEM_WIDTH
IN_SPLITS = (MLSTM_WIDTH, MLSTM_WIDTH, MOBA_WIDTH, MOBA_WIDTH, MOBA_WIDTH, MOBA_WIDTH, MEM_WIDTH, MEM_WIDTH)
IN_WIDTH = 2 * MLSTM_WIDTH + 4 * MOBA_WIDTH + 2 * MEM_WIDTH

ROPE_THETA = 10000.0
DEEPNORM_ALPHA = (2 * DEPTH) ** 0.25
DEEPNORM_BETA = (8 * DEPTH) ** -0.25
LN_EPS = 1e-5

kernel_name = 'hymba_mlstm_moba_memxattn_deepnorm'


def _split_cols(a, sizes):
    outs, off = [], 0
    for s in sizes:
        outs.append(a[..., off:off + s])
        off += s
    return outs


def _layer_norm(x, g, b):
    xf = x.astype(jnp.float32)
    mu = jnp.mean(xf, axis=-1, keepdims=True)
    var = jnp.mean(jnp.square(xf - mu), axis=-1, keepdims=True)
    return (xf - mu) * lax.rsqrt(var + LN_EPS) * g.astype(jnp.float32) + b.astype(jnp.float32)


def _rotary(x, positions):
    d = x.shape[-1]
    half = d // 2
    inv_freq = ROPE_THETA ** (-jnp.arange(half, dtype=jnp.float32) * 2.0 / d)
    ang = positions.astype(jnp.float32)[..., None] * inv_freq
    cos = jnp.cos(ang)[:, :, None, :]
    sin = jnp.sin(ang)[:, :, None, :]
    xf = x.astype(jnp.float32)
    x1, x2 = xf[..., :half], xf[..., half:]
    return jnp.concatenate([x1 * cos - x2 * sin, x2 * cos + x1 * sin], axis=-1).astype(x.dtype)


def _causal_conv(x, w, b):
    k_w = w.shape[0]
    s = x.shape[1]
    xp = jnp.pad(x, ((0, 0), (k_w - 1, 0), (0, 0)))
    out = b
    for j in range(k_w):
        out = out + xp[:, j:j + s] * w[j]
    return out


def _blockdiag(x, w):
    g = w.shape[0]
    xs = x.reshape(*x.shape[:-1], g, -1)
    return jnp.einsum('bsgi,gio->bsgo', xs, w).reshape(x.shape)


def _mlstm_chunkwise(q, k, v, i_pre, log_f):
    bsz, nh, s, d = q.shape
    L = MLSTM_CHUNK
    nc = s // L

    def to_chunks(a):
        return jnp.moveaxis(a.reshape(bsz, nh, nc, L, *a.shape[3:]), 2, 0)

    causal = jnp.tril(jnp.ones((L, L), dtype=bool))

    def step(carry, inp):
        c_prev, n_prev, m_prev = carry
        qc, kc, vc, ic, fc = inp
        b = jnp.cumsum(fc, axis=-1)
        log_d = jnp.where(causal, b[..., :, None] - b[..., None, :] + ic[..., None, :], -jnp.inf)
        m_inter = b + m_prev[..., None]
        m_row = jnp.maximum(m_inter, jnp.max(log_d, axis=-1))
        w_inter = jnp.exp(m_inter - m_row)
        s_qk = jnp.einsum('bhld,bhsd->bhls', qc, kc) * jnp.exp(log_d - m_row[..., None])
        num = (w_inter[..., None] * jnp.einsum('bhld,bhde->bhle', qc, c_prev)
               + jnp.einsum('bhls,bhse->bhle', s_qk, vc))
        den = w_inter * jnp.einsum('bhld,bhd->bhl', qc, n_prev) + jnp.sum(s_qk, axis=-1)
        h = num / jnp.maximum(jnp.abs(den), jnp.exp(-m_row))[..., None]
        b_end = b[..., -1]
        log_w = b_end[..., None] - b + ic
        m_new = jnp.maximum(b_end + m_prev, jnp.max(log_w, axis=-1))
        decay = jnp.exp(b_end + m_prev - m_new)
        w_in = jnp.exp(log_w - m_new[..., None])
        c_new = decay[..., None, None] * c_prev + jnp.einsum('bhl,bhld,bhle->bhde', w_in, kc, vc)
        n_new = decay[..., None] * n_prev + jnp.einsum('bhl,bhld->bhd', w_in, kc)
        return (c_new, n_new, m_new), h

    init = (jnp.zeros((bsz, nh, d, d), q.dtype), jnp.zeros((bsz, nh, d), q.dtype),
            jnp.zeros((bsz, nh), q.dtype))
    _, h = lax.scan(step, init, (to_chunks(q), to_chunks(k), to_chunks(v),
                                 to_chunks(i_pre), to_chunks(log_f)))
    return jnp.moveaxis(h, 0, 2).reshape(bsz, nh, s, d)


def _moba_attention(q, k, v):
    bsz, s, nh, d = q.shape
    bs = MOBA_BLOCK
    qg = MOBA_QUERY_GROUP
    nb = -(-s // bs)
    sp = nb * bs
    pad = sp - s
    padw = ((0, 0), (0, 0), (0, pad), (0, 0))
    qh = jnp.pad(q.transpose(0, 2, 1, 3), padw).astype(jnp.float32)
    kh = jnp.pad(k.transpose(0, 2, 1, 3), padw).astype(jnp.float32)
    vh = jnp.pad(v.transpose(0, 2, 1, 3), padw).astype(jnp.float32)
    k_blocks = kh.reshape(bsz, nh, nb, bs, d)
    v_blocks = vh.reshape(bsz, nh, nb, bs, d)
    scale = d ** -0.5
    n_sel = min(MOBA_TOP_K, nb - 1)
    qblk_all = jnp.arange(sp) // bs

    if n_sel > 0:
        k_mean = jnp.mean(k_blocks, axis=3)
        gate = jnp.einsum('bhtd,bhnd->bhtn', qh, k_mean)
        fully_past = jnp.arange(nb)[None, :] < qblk_all[:, None]
        gate = jnp.where(fully_past, gate, -jnp.inf)
        _, sel = lax.top_k(gate, n_sel)
        sel_valid = sel < qblk_all[:, None]
    b_ix = jnp.arange(bsz)[:, None, None, None]
    h_ix = jnp.arange(nh)[None, :, None, None]

    def group(c):
        start = c * qg
        blk = start // bs
        q_c = lax.dynamic_slice_in_dim(qh, start, qg, axis=2)
        k_own = lax.dynamic_index_in_dim(k_blocks, blk, axis=2, keepdims=False)
        v_own = lax.dynamic_index_in_dim(v_blocks, blk, axis=2, keepdims=False)
        qpos = start + jnp.arange(qg)
        kpos = blk * bs + jnp.arange(bs)
        s_own = jnp.einsum('bhqd,bhkd->bhqk', q_c, k_own) * scale
        s_own = jnp.where(kpos[None, :] <= qpos[:, None], s_own, -jnp.inf)
        if n_sel > 0:
            idx = lax.dynamic_slice_in_dim(sel, start, qg, axis=2)
            ok = lax.dynamic_slice_in_dim(sel_valid, start, qg, axis=2)
            k_sel = k_blocks[b_ix, h_ix, idx]
            v_sel = v_blocks[b_ix, h_ix, idx]
            s_past = jnp.einsum('bhqd,bhqnkd->bhqnk', q_c, k_sel) * scale
            s_past = jnp.where(ok[..., None], s_past, -jnp.inf).reshape(bsz, nh, qg, n_sel * bs)
            p = jax.nn.softmax(jnp.concatenate([s_past, s_own], axis=-1), axis=-1)
            p_past = p[..., :n_sel * bs].reshape(bsz, nh, qg, n_sel, bs)
            p_own = p[..., n_sel * bs:]
            out = (jnp.einsum('bhqnk,bhqnkd->bhqd', p_past, v_sel)
                   + jnp.einsum('bhqk,bhkd->bhqd', p_own, v_own))
        else:
            p_own = jax.nn.softmax(s_own, axis=-1)
            out = jnp.einsum('bhqk,bhkd->bhqd', p_own, v_own)
        return out

    out = lax.map(group, jnp.arange(sp // qg))
    out = jnp.moveaxis(out, 0, 2).reshape(bsz, nh, sp, d)[:, :, :s]
    return out.transpose(0, 2, 1, 3).astype(q.dtype)


def _memory_attention(q, mem, w_mem_kv):
    bsz, m, _ = mem.shape
    kv = mem @ w_mem_kv
    k_m, v_m = _split_cols(kv, (MEM_WIDTH, MEM_WIDTH))
    k_m = k_m.reshape(bsz, m, MEM_HEADS, MEM_HEAD_DIM).astype(jnp.float32)
    v_m = v_m.reshape(bsz, m, MEM_HEADS, MEM_HEAD_DIM).astype(jnp.float32)
    s = jnp.einsum('bshd,bmhd->bhsm', q.astype(jnp.float32), k_m) * (MEM_HEAD_DIM ** -0.5)
    p = jax.nn.softmax(s, axis=-1)
    return jnp.einsum('bhsm,bmhd->bshd', p, v_m).astype(q.dtype)


def setup_inputs(seed: int = 0) -> dict:
    key = jax.random.key(seed)
    ks = jax.random.split(key, 20)
    f32 = jnp.float32
    nrm = lambda k, shp: jax.random.normal(k, shp, dtype=f32)
    x = nrm(ks[0], (BATCH, SEQ, D_MODEL))
    mem = nrm(ks[1], (BATCH, MEM_TOKENS, D_MODEL))
    positions = jnp.broadcast_to(jnp.arange(SEQ, dtype=jnp.int32)[None, :], (BATCH, SEQ))
    w_in = nrm(ks[2], (D_MODEL, IN_WIDTH)) * D_MODEL ** -0.5
    mlstm_conv_w = nrm(ks[3], (MLSTM_CONV, MLSTM_WIDTH)) * MLSTM_CONV ** -0.5
    mlstm_conv_b = 0.01 * nrm(ks[4], (MLSTM_WIDTH,))
    nblk = MLSTM_WIDTH // MLSTM_QKV_BLOCK
    bshape = (nblk, MLSTM_QKV_BLOCK, MLSTM_QKV_BLOCK)
    mlstm_wq = nrm(ks[5], bshape) * MLSTM_QKV_BLOCK ** -0.5
    mlstm_wk = nrm(ks[6], bshape) * MLSTM_QKV_BLOCK ** -0.5
    mlstm_wv = nrm(ks[7], bshape) * MLSTM_QKV_BLOCK ** -0.5
    mlstm_w_gates = nrm(ks[8], (3 * MLSTM_WIDTH, 2 * MLSTM_HEADS)) * (3 * MLSTM_WIDTH) ** -0.5
    b_i = 0.1 * nrm(ks[9], (MLSTM_HEADS,))
    b_f = jnp.linspace(3.0, 6.0, MLSTM_HEADS, dtype=f32) + 0.01 * nrm(ks[10], (MLSTM_HEADS,))
    mlstm_b_gates = jnp.concatenate([b_i, b_f])
    mlstm_norm_g = 1.0 + 0.02 * nrm(ks[11], (MLSTM_WIDTH,))
    mlstm_skip = 1.0 + 0.02 * nrm(ks[12], (MLSTM_WIDTH,))
    w_mem_kv = nrm(ks[13], (D_MODEL, 2 * MEM_WIDTH)) * D_MODEL ** -0.5
    w_out = nrm(ks[14], (MIX_WIDTH, D_MODEL)) * MIX_WIDTH ** -0.5 * DEEPNORM_BETA
    ln_g = 1.0 + 0.02 * nrm(ks[15], (D_MODEL,))
    ln_b = 0.02 * nrm(ks[16], (D_MODEL,))
    return {'x': x, 'mem': mem, 'positions': positions, 'w_in': w_in,
            'mlstm_conv_w': mlstm_conv_w, 'mlstm_conv_b': mlstm_conv_b,
            'mlstm_wq': mlstm_wq, 'mlstm_wk': mlstm_wk, 'mlstm_wv': mlstm_wv,
            'mlstm_w_gates': mlstm_w_gates, 'mlstm_b_gates': mlstm_b_gates,
            'mlstm_norm_g': mlstm_norm_g, 'mlstm_skip': mlstm_skip,
            'w_mem_kv': w_mem_kv, 'w_out': w_out, 'ln_g': ln_g, 'ln_b': ln_b}


def reference(x, mem, positions, w_in, mlstm_conv_w, mlstm_conv_b, mlstm_wq, mlstm_wk, mlstm_wv,
              mlstm_w_gates, mlstm_b_gates, mlstm_norm_g, mlstm_skip, w_mem_kv, w_out, ln_g, ln_b):
    bsz, s, _ = x.shape
    for _layer in range(DEPTH):
        proj = x @ w_in
        x_m, z_m, q_a, k_a, v_a, z_a, q_c, z_c = _split_cols(proj, IN_SPLITS)

        x_conv = jax.nn.silu(_causal_conv(x_m, mlstm_conv_w, mlstm_conv_b))
        q_m = _blockdiag(x_conv, mlstm_wq)
        k_m = _blockdiag(x_conv, mlstm_wk)
        v_m = _blockdiag(x_m, mlstm_wv)
        gates = jnp.concatenate([q_m, k_m, v_m], axis=-1) @ mlstm_w_gates + mlstm_b_gates
        gates = gates.astype(jnp.float32).transpose(0, 2, 1)
        i_pre, f_pre = gates[:, :MLSTM_HEADS], gates[:, MLSTM_HEADS:]
        heads = lambda a: a.reshape(bsz, s, MLSTM_HEADS, MLSTM_HEAD_DIM).transpose(0, 2, 1, 3).astype(jnp.float32)
        h = _mlstm_chunkwise(heads(q_m), heads(k_m) * (MLSTM_HEAD_DIM ** -0.5), heads(v_m),
                             i_pre, jax.nn.log_sigmoid(f_pre))
        mu = jnp.mean(h, axis=-1, keepdims=True)
        var = jnp.mean(jnp.square(h - mu), axis=-1, keepdims=True)
        h = ((h - mu) * lax.rsqrt(var + LN_EPS)).transpose(0, 2, 1, 3).reshape(bsz, s, MLSTM_WIDTH)
        h = (h * mlstm_norm_g.astype(jnp.float32)).astype(x.dtype)
        out_m = (h + mlstm_skip * x_conv) * jax.nn.silu(z_m)

        qa = _rotary(q_a.reshape(bsz, s, MOBA_HEADS, MOBA_HEAD_DIM), positions)
        ka = _rotary(k_a.reshape(bsz, s, MOBA_HEADS, MOBA_HEAD_DIM), positions)
        va = v_a.reshape(bsz, s, MOBA_HEADS, MOBA_HEAD_DIM)
        out_a = _moba_attention(qa, ka, va).reshape(bsz, s, MOBA_WIDTH) * jax.nn.silu(z_a)

        qc = q_c.reshape(bsz, s, MEM_HEADS, MEM_HEAD_DIM)
        out_c = _memory_attention(qc, mem, w_mem_kv).reshape(bsz, s, MEM_WIDTH) * jax.nn.silu(z_c)

        mixed = jnp.concatenate([out_m, out_a, out_c], axis=-1) @ w_out
        x = _layer_norm(DEEPNORM_ALPHA * x + mixed, ln_g, ln_b).astype(x.dtype)
    return x
```

```python
import numpy as np
from contextlib import ExitStack
import concourse.bass as bass
import concourse.mybir as mybir
from concourse.bass_utils import run_bass_kernel_spmd

F32 = mybir.dt.float32
BF16 = mybir.dt.bfloat16
I32 = mybir.dt.int32
AF = mybir.ActivationFunctionType
ALU = mybir.AluOpType
AX = mybir.AxisListType

P = 128
NT = 2048
DM = 1024
INW = 5120
ALPHA = 2.0 ** 0.25
LN_EPS = 1e-5
SC128 = 128.0 ** -0.5
SC256 = 256.0 ** -0.5


class Sched:
    ENG = ['pe', 'act', 'dve', 'pool', 'sp']
    BLK = {'pe': 'tensor', 'act': 'scalar', 'dve': 'vector', 'pool': 'gpsimd', 'sp': 'sync'}

    def __init__(self, nc, es, nds=8):
        self.nc = nc
        self.sem = {}
        self.ccnt = {}
        for e in ['pe', 'act', 'dve', 'pool']:
            self.sem['c_' + e] = es.enter_context(nc.semaphore('c_' + e))
            self.ccnt[e] = 0
        self.nds = nds
        self.dcnt = {}
        self.drr = {}
        for q in ['sp', 'pool']:
            self.drr[q] = 0
            for i in range(nds):
                k = 'd_%s%d' % (q, i)
                self.sem[k] = es.enter_context(nc.semaphore(k))
                self.dcnt[k] = 0
        self.known = {e: {} for e in self.ENG}
        self.lastw = {}
        self.readers = {}
        self.ops = {e: [] for e in self.ENG}
        self.nops = 0

    def add(self, eng, fn, reads=(), writes=(), dma=False):
        waits = {}
        writes = list(writes) + [t for t in reads if isinstance(t, tuple) and t[0] == 'ps' and t not in writes]

        def need(tok, raw):
            k, v, teng, tdma = tok
            if not dma and not tdma and teng == eng and eng == 'pe':
                return
            if self.known[eng].get(k, 0) >= v:
                return
            if waits.get(k, 0) < v:
                waits[k] = v

        for t in reads:
            tok = self.lastw.get(t)
            if tok is not None:
                need(tok, True)
        for t in writes:
            tok = self.lastw.get(t)
            if tok is not None:
                need(tok, False)
            for tok in self.readers.get(t, ()):
                need(tok, False)
        if dma:
            i = self.drr[eng]
            self.drr[eng] = (i + 1) % self.nds
            k = 'd_%s%d' % (eng, i)
            prev = self.dcnt[k]
            if prev > 0 and self.known[eng].get(k, 0) < prev:
                waits[k] = max(waits.get(k, 0), prev)
            self.dcnt[k] = prev + 16
            tok = (k, prev + 16, eng, True)
            inc = (k, 16)
        else:
            self.ccnt[eng] += 1
            tok = ('c_' + eng, self.ccnt[eng], eng, False)
            inc = ('c_' + eng, 1)
        for k, v in waits.items():
            self.known[eng][k] = v
        for t in writes:
            self.lastw[t] = tok
            self.readers[t] = []
        for t in reads:
            lst = self.readers.setdefault(t, [])
            lst[:] = [x for x in lst if x[0] != tok[0]]
            lst.append(tok)
        self.ops[eng].append((list(waits.items()), fn, inc))
        self.nops += 1

    def barrier(self):
        allv = {}
        for e in ['pe', 'act', 'dve', 'pool']:
            if self.ccnt[e] > 0:
                allv['c_' + e] = self.ccnt[e]
        for k, v in self.dcnt.items():
            if v > 0:
                allv[k] = v
        for e in self.ENG:
            w = []
            for k, v in allv.items():
                if self.known[e].get(k, 0) < v:
                    w.append((k, v))
                    self.known[e][k] = v
            if w:
                self.ops[e].append((w, None, None))
        self.lastw = {}
        self.readers = {}

    def flush(self):
        sem = self.sem
        with self.nc.Block() as block:
            for e in self.ENG:
                ops = self.ops[e]
                if not ops:
                    continue

                def body(engine, ops=ops):
                    for waits, fn, inc in ops:
                        for k, v in waits:
                            engine.wait_ge(sem[k], v)
                        if fn is not None:
                            ins = fn(engine)
                            ins.then_inc(sem[inc[0]], inc[1])

                getattr(block, self.BLK[e])(body)
        self.ops = {e: [] for e in self.ENG}


def build_nc(debug=False):
    nc = bass.Bass("TRN2", target_bir_lowering=False)

    def din(name, shape, dt=F32):
        return nc.dram_tensor(name, list(shape), dt, kind="ExternalInput").ap()

    xT_d = din("xT", [DM, NT])
    x_d = din("x", [NT, DM])
    memT_d = din("memT", [DM, 256])
    pos_d = din("pos", [1, NT], I32)
    w_in_d = din("w_in", [DM, INW])
    w_kv_d = din("w_kv", [DM, 1024])
    w_out_d = din("w_out", [2048, DM])
    convw_d = din("convw", [P, 8, 4])
    convb_d = din("convb", [P, 8])
    wq_d = din("wq_bd", [P, 8, P])
    wk_d = din("wk_bd", [P, 8, P])
    wv_d = din("wv_bd", [P, 8, P])
    wg_d = din("wg", [P, 24, 8])
    wqT_d = din("wqT_bd", [P, 8, P])
    wkT_d = din("wkT_bd", [P, 8, P])
    wvT_d = din("wvT_bd", [P, 8, P])
    bg_d = din("bg", [8, 1])
    ng_d = din("normg", [P, 8])
    sk_d = din("skip", [P, 8])
    lng_d = din("ln_g", [1, DM])
    lnb_d = din("ln_b", [1, DM])
    invf_d = din("invf", [P, 1])
    y_d = nc.dram_tensor("y", [NT, DM], F32, kind="ExternalOutput").ap()
    if debug:
        dbg_cat = nc.dram_tensor("dbg_cat", [P, 16, NT], BF16, kind="ExternalOutput").ap()

    w_in_v = w_in_d.rearrange("(kc p) n -> p kc n", p=P)
    w_kv_v = w_kv_d.rearrange("(kc p) n -> p kc n", p=P)
    w_out_v = w_out_d.rearrange("(kc p) n -> p kc n", p=P)
    xT_v = xT_d.rearrange("(kc p) t -> p kc t", p=P)
    memT_v = memT_d.rearrange("(kc p) t -> p kc t", p=P)

    with ExitStack() as es:
        S = Sched(nc, es)
        ARENA_BYTES = 200 * 1024
        arena = es.enter_context(nc.sbuf_tensor("arena", [P, ARENA_BYTES // 2], BF16))
        ps = [es.enter_context(nc.psum_tensor("ps%d" % k, [P, 512], F32)) for k in range(8)]

        def view(off, shape, dt=BF16):
            esz = 2 if dt == BF16 else 4
            n = 1
            for s_ in shape[1:]:
                n *= s_
            nbytes = n * esz
            assert off % 4 == 0 and off + nbytes <= ARENA_BYTES, (off, nbytes)
            ap = arena[0:shape[0], off // 2: (off + nbytes) // 2]
            if dt != BF16:
                ap = ap.bitcast(dt)
            if len(shape) == 3:
                ap = ap.rearrange("p (a b) -> p a b", a=shape[1])
            elif len(shape) == 4:
                ap = ap.rearrange("p (a b c) -> p a b c", a=shape[1], b=shape[2])
            return ap

        OFF_CAT = 0
        OFF_XT = 65536
        OFF_XM = OFF_XT + 32768
        OFF_R = OFF_XM + 32816 + 16
        cat = view(OFF_CAT, [P, 16, NT])
        xT = view(OFF_XT, [P, 8, NT])

        def sb(name, shape, dt):
            return es.enter_context(nc.sbuf_tensor(name, shape, dt))

        ident = sb("ident", [P, P], BF16)
        ident32 = sb("ident32", [P, P], F32)
        onesb = sb("onesb", [P, P], BF16)
        ones32 = sb("ones32", [P, P], F32)
        U32 = sb("U32", [P, P], F32)
        triT = sb("triT", [P, P], BF16)
        triAB = sb("triAB", [P, 512], BF16)
        esel = sb("esel", [8, 8, P], BF16)
        one1 = sb("one1", [P, 1], F32)
        lnhalf = sb("lnhalf", [P, 1], F32)
        bg = sb("bg_s", [8, 1], F32)
        invf = sb("invf_s", [P, 1], F32)
        convw = sb("convw_s", [P, 8, 4], F32)
        convb = sb("convb_s", [P, 8], F32)
        normg = sb("normg_s", [P, 8], F32)
        skipv = sb("skip_s", [P, 8], F32)
        negm = sb("negm", [P, 4, 4, 2, 8], F32)

        cnt = [0]

        def uid():
            cnt[0] += 1
            return cnt[0]

        def mm(out, lhsT, rhs, start, stop, reads, writes):
            S.add('pe', lambda e: e.matmul(out, lhsT, rhs, start=start, stop=stop), reads, writes)

        def act(out, in_, func, reads, writes, eng='act', **kw):
            S.add(eng, lambda e: e.activation(out=out, in_=in_, func=func, **kw), reads, writes)

        def dma(q, out, in_, reads, writes):
            S.add(q, lambda e: e.dma_start(out=out, in_=in_), reads, writes, dma=True)

        dma('pool', view(OFF_CAT + 16384, [P, NT], F32), pos_d.partition_broadcast(P), [], ['tB'])
        dma('pool', xT[:, :, 0:512], xT_v[:, :, 0:512], [], [('xT', 0)])
        for dst, src, nm in [(bg, bg_d, 'bg'), (invf, invf_d, 'invf'), (convw, convw_d, 'convw'), (convb, convb_d, 'convb'),
                             (normg, ng_d, 'normg'), (skipv, sk_d, 'skipv')]:
            dma('sp', dst[:], src, [], [nm])
        def build_consts():
            G = 'pool'
            S.add(G, lambda e: e.memset(ident[:], 1.0), [], ['ident'])
            S.add(G, lambda e: e.affine_select(out=ident[:], in_=ident[:], pattern=[[1, P]], compare_op=ALU.is_equal, fill=0.0,
                                               base=0, channel_multiplier=-1), ['ident'], ['ident'])
            S.add(G, lambda e: e.memset(ident32[:], 1.0), [], ['ident32'])
            S.add(G, lambda e: e.affine_select(out=ident32[:], in_=ident32[:], pattern=[[1, P]], compare_op=ALU.is_equal, fill=0.0,
                                               base=0, channel_multiplier=-1), ['ident32'], ['ident32'])
            S.add(G, lambda e: e.memset(onesb[:], 1.0), [], ['onesb'])
            S.add(G, lambda e: e.memset(ones32[:], 1.0), [], ['ones32'])
            S.add(G, lambda e: e.memset(one1[:], 1.0), [], ['one1'])
            S.add(G, lambda e: e.memset(lnhalf[:], float(np.log(0.5))), [], ['lnhalf'])
            S.add(G, lambda e: e.memset(U32[:], 1.0), [], ['U32'])
            S.add(G, lambda e: e.affine_select(out=U32[:], in_=U32[:], pattern=[[1, P]], compare_op=ALU.is_ge, fill=0.0,
                                               base=0, channel_multiplier=-1), ['U32'], ['U32'])
            S.add(G, lambda e: e.memset(triT[:], 1.0), [], ['triT'])
            S.add(G, lambda e: e.affine_select(out=triT[:], in_=triT[:], pattern=[[1, P]], compare_op=ALU.is_ge, fill=0.0,
                                               base=0, channel_multiplier=-1), ['triT'], ['triT'])
            S.add(G, lambda e: e.memset(triAB[:], 1.0), [], ['triAB'])
            S.add(G, lambda e: e.affine_select(out=triAB[:, 0:256], in_=triAB[:, 0:256], pattern=[[1, 256]], compare_op=ALU.is_ge,
                                               fill=0.0, base=0, channel_multiplier=-1), ['triAB'], ['triAB'])
            S.add(G, lambda e: e.affine_select(out=triAB[:, 256:512], in_=triAB[:, 256:512], pattern=[[1, 256]], compare_op=ALU.is_ge,
                                               fill=0.0, base=-128, channel_multiplier=-1), ['triAB'], ['triAB'])
            S.add(G, lambda e: e.tensor_scalar(triAB[:], triAB[:], -1.0, 30000.0, ALU.add, ALU.mult), ['triAB'], ['triAB'])
            S.add(G, lambda e: e.memset(esel[:], 1.0), [], ['esel'])
            S.add(G, lambda e: e.affine_select(out=esel[:], in_=esel[:], pattern=[[1, 8], [0, P]], compare_op=ALU.is_equal, fill=0.0,
                                               base=0, channel_multiplier=-1), ['esel'], ['esel'])
            S.add(G, lambda e: e.memset(negm[:], 0.0), [], ['negm'])
            for j in range(4, 8):
                S.add(G, lambda e, j=j: e.memset(negm[:, :, j - 4, :, j:8], -1e30), ['negm'], ['negm'])


        bank_rr = [0]

        def next_bank(lo, hi):
            b = lo + (bank_rr[0] % (hi - lo))
            bank_rr[0] += 1
            return b

        def inproj_fm(wbuf, wkey, evac, nchunk=4, banks=(0, 4), jouter=False):
            order = [(c, j) for j in range(4) for c in range(nchunk)] if jouter else [(c, j) for c in range(nchunk) for j in range(4)]
            for c, j in order:
                b = next_bank(*banks)
                for kc in range(8):
                    mm(ps[b][:], wbuf[:, kc, c * P:(c + 1) * P], xT[:, kc, j * 512:(j + 1) * 512], kc == 0, kc == 7,
                       [wkey, ('xT', j)], [('ps', b)])
                evac(c, j, ps[b], ('ps', b))

        evac_rr = [0]

        def copy_evac(dst_ap, src_ap, reads, writes):
            evac_rr[0] += 1
            if evac_rr[0] % 2 == 0:
                S.add('act', lambda e: e.copy(dst_ap, src_ap), reads, writes)
            else:
                S.add('dve', lambda e: e.tensor_copy(dst_ap, src_ap), reads, writes)

        cosB = view(OFF_CAT, [P, NT], F32)
        sinS = view(OFF_CAT + 8192, [P, NT], F32)
        tB = view(OFF_CAT + 16384, [P, NT], F32)
        tC = view(OFF_CAT + 24576, [P, NT], F32)
        tA = view(OFF_XM, [P, NT], I32)
        V = 'dve'
        S.add(V, lambda e: e.tensor_scalar(tB, tB, invf[:, 0:1], None, ALU.mult), ['tB', 'invf'], ['tB'])
        S.add(V, lambda e: e.tensor_copy(tA, tB), ['tB'], ['tA'])
        S.add(V, lambda e: e.tensor_copy(tC, tA), ['tA'], ['tC'])
        S.add(V, lambda e: e.tensor_tensor(tC, tB, tC, ALU.subtract), ['tB', 'tC'], ['tC'])
        act(sinS[0:64, :], tC[0:64, :], AF.Sin, ['tC'], ['sinS0'], scale=-2.0 * np.pi)
        act(sinS[64:128, :], tC[64:128, :], AF.Sin, ['tC'], ['sinS1'], scale=2.0 * np.pi)
        S.add(V, lambda e: e.tensor_scalar(tB, tB, 0.25, None, ALU.add), ['tB'], ['tB'])
        S.add(V, lambda e: e.tensor_copy(tA, tB), ['tB'], ['tA'])
        S.add(V, lambda e: e.tensor_copy(tC, tA), ['tA'], ['tC'])
        S.add(V, lambda e: e.tensor_tensor(tC, tB, tC, ALU.subtract), ['tB', 'tC'], ['tC'])
        act(cosB, tC, AF.Sin, ['tC'], ['cosB'], scale=2.0 * np.pi)

        def ev_silu(cbase):
            def f(c, j, pst, pkey):
                dst = cat[:, cbase + c, j * 512:(j + 1) * 512]
                act(dst, pst[:], AF.Silu, [pkey], [('cat', cbase + c, j)])
            return f

        o = OFF_R
        wA = view(o, [P, 8, 512]); o += 8192
        wB = view(o, [P, 8, 512]); o += 8192
        wC = view(o, [P, 8, 512]); o += 8192
        wD = view(o, [P, 8, 512]); o += 8192
        va = view(o, [P, 16, 512]); o += 16384
        rt1 = view(o, [P, 2, 512], F32)
        selT = view(o, [8, 4, 1024])
        rt2 = view(o + 4096, [P, 2, 512], F32); o += 8192
        Ebuf = view(o, [P, 4, 512]); o += 4096
        Esum = view(o, [P, 2, 512], F32); o += 4096
        km32 = view(o, [P, 4, 8], F32); o += 128
        kmb = view(o, [P, 4, 8]); o += 64
        gsb = view(o, [P, 32, 8], F32); o += 1024
        top8 = view(o, [P, 32, 8], F32); o += 1024
        selb = view(o, [P, 32, 8]); o += 512
        thb = view(o, [P, 2, 512], F32)
        Ehl = view(o, [P, 4, 512]); o += 4096
        assert o <= ARENA_BYTES, o
        qc = view(OFF_R + 16384, [P, 4, NT])
        rden2 = view(OFF_CAT + 16384, [P, 2, 512], F32)
        t2c = view(OFF_CAT + 16384 + 4096, [P, 2, 512], F32)
        osb = view(OFF_CAT + 16384 + 8192, [P, 2, 512], F32)
        qa = view(OFF_XM, [P, 4, NT])
        ka = view(OFF_XM + 16384, [P, 4, NT])
        rtk = [('rt1', 0), ('rt1', 1), ('rt2', 0, 0), ('rt2', 0, 1), ('rt2', 1, 0), ('rt2', 1, 1)]

        dma('pool', wA[:], w_in_v[:, :, 2048:2560], [], ['wA'])
        dma('pool', wB[:], w_in_v[:, :, 3072:3584], [], ['wB'])
        dma('pool', wC[:], w_in_v[:, :, 2560:3072], [], ['wC'])
        dma('pool', wD[:], w_in_v[:, :, 3584:4096], [], ['wD'])
        for j in range(1, 4):
            dma('pool', xT[:, :, j * 512:(j + 1) * 512], xT_v[:, :, j * 512:(j + 1) * 512], [], [('xT', j)])
        build_consts()
        rr2 = [0]

        def ev_rot(dst, name):
            def f(c, j, pst, pkey):
                k = rr2[0] % 2
                rr2[0] += 1
                sl = slice(j * 512, (j + 1) * 512)
                S.add('dve', lambda e: e.tensor_tensor(rt1[:, k, :], pst[:], cosB[:, sl], ALU.mult), [pkey, 'cosB'], [('rt1', k)])
                S.add('dve', lambda e: e.tensor_tensor(rt2[0:64, k, :], pst[64:128, :], sinS[0:64, sl], ALU.mult), [pkey, 'sinS0'], [('rt2', k, 0)])
                S.add('dve', lambda e: e.tensor_tensor(rt2[64:128, k, :], pst[0:64, :], sinS[64:128, sl], ALU.mult), [pkey, 'sinS1'], [('rt2', k, 1)])
                S.add('pool', lambda e: e.tensor_tensor(dst[:, c, sl], rt1[:, k, :], rt2[:, k, :], ALU.add),
                      [('rt1', k), ('rt2', k, 0), ('rt2', k, 1)], [(name, c, j)])
            return f

        def fm_tile(wbuf, wkey, evac, c, j):
            b = next_bank(0, 4)
            for kc in range(8):
                mm(ps[b][:], wbuf[:, kc, c * P:(c + 1) * P], xT[:, kc, j * 512:(j + 1) * 512], kc == 0, kc == 7,
                   [wkey, ('xT', j)], [('ps', b)])
            evac(c, j, ps[b], ('ps', b))

        def va_tile(i):
            b = next_bank(0, 4)
            for kc in range(8):
                mm(ps[b][:], xT[:, kc, i * P:(i + 1) * P], wB[:, kc, :], kc == 0, kc == 7, ['wB', ('xT', i // 4)], [('ps', b)])
            S.add('act', lambda e, i=i, b=b: e.copy(va[:, i, :], ps[b][:]), [('ps', b)], [('va', i)])

        evq, evk, evz = ev_rot(qa, 'qa'), ev_rot(ka, 'ka'), ev_silu(8)
        for c in range(4):
            fm_tile(wA, 'wA', evq, c, 0)
        for c in range(4):
            va_tile(c)
        for c in range(4):
            fm_tile(wC, 'wC', evk, c, 0)
            fm_tile(wD, 'wD', evz, c, 0)
        for j in range(1, 4):
            for c in range(4):
                fm_tile(wA, 'wA', evq, c, j)
                va_tile(j * 4 + c)
                fm_tile(wC, 'wC', evk, c, j)
                fm_tile(wD, 'wD', evz, c, j)
        sel_ops = []
        for h in range(4):
            sel_ops.append(lambda h=h: S.add('dve', lambda e: e.tensor_reduce(km32[:, h, :], ka[:, h, :].rearrange("p (n k) -> p n k", n=8), AX.X, ALU.add),
                                             [('ka', h, j) for j in range(4)], [('km32', h)]))
            sel_ops.append(lambda h=h: S.add('dve', lambda e: e.tensor_scalar(kmb[:, h, :], km32[:, h, :], 1.0 / 256.0, None, ALU.mult), [('km32', h)], [('kmb', h)]))
        for h in range(4):
            for qi in range(8, 16):
                g_ = h * 8 + (qi - 8)
                sel_ops.append(lambda h=h, qi=qi, g_=g_: mm(ps[3][:, g_ * 8:(g_ + 1) * 8], qa[:, h, qi * P:(qi + 1) * P], kmb[:, h, :], True, True,
                                                            [('qa', h, qi // 4), ('kmb', h)], [('ps', 3)]))
        sel_ops.append(lambda: S.add('dve', lambda e: e.tensor_tensor(gsb[:], ps[3][:, 0:256].rearrange("p (g c) -> p g c", c=8),
                                                                      negm[:].rearrange("p h j r c -> p (h j r) c"), ALU.add), [('ps', 3), 'negm'], ['gsb']))
        for g_ in range(32):
            sel_ops.append(lambda g_=g_: S.add('dve', lambda e: e.max(out=top8[:, g_, :], in_=gsb[:, g_, :]), ['gsb'], ['top8']))
        sel_ops.append(lambda: S.add('dve', lambda e: e.tensor_tensor(gsb[:], gsb[:], top8[:, :, 2:3].to_broadcast([P, 32, 8]), ALU.is_ge), ['gsb', 'top8'], ['gsb']))
        sel_ops.append(lambda: S.add('dve', lambda e: e.tensor_scalar(selb[:], gsb[:], -1.0, 30000.0, ALU.add, ALU.mult), ['gsb'], ['selb']))
        for h in range(4):
            pT_ = ps[3][:].bitcast(BF16)
            for q8 in range(8):
                sel_ops.append(lambda h=h, q8=q8, pT_=pT_: S.add('pe', lambda e: e.transpose(pT_[0:8, q8 * P:(q8 + 1) * P], selb[:, h * 8 + q8, :], ident[:]),
                                                                 ['selb', 'ident'], [('ps', 3)]))
            sel_ops.append(lambda h=h, pT_=pT_: S.add('act', lambda e: e.copy(selT[0:8, h, :], pT_[0:8, :]), [('ps', 3)], [('selT', h)] + rtk))
        sel_state = [0]

        def selection(nops):
            for _ in range(nops):
                if sel_state[0] < len(sel_ops):
                    sel_ops[sel_state[0]]()
                    sel_state[0] += 1

        items = []
        for T in range(4):
            for h in range(4):
                for n in range(2 * T + 1):
                    items.append(('w', h, T, n, 0))
                    items.append(('w', h, T, n, 1))
                items.append(('n', h, T, 2 * T + 1, 0))
        NB = 3

        def warm(nd):
            for _ in range(nd):
                mm(ps[3][:], ident[:], triAB[:], True, True, ['ident', 'triAB'], [('ps', 3)])

        fill_units = []

        def add_fill_group(wbuf, wkey, kind, cbase):
            for j in range(4):
                for c in range(4):
                    fill_units.append((wbuf, wkey, kind, cbase, c, j))

        add_fill_group(wA, 'wA', 'qc', 0)
        nfirst = 10 ** 9
        fstate = {'u': 0, 'kc': 0, 'n': 0}

        def fill(nmm):
            for _ in range(nmm):
                if fstate['u'] >= len(fill_units):
                    return
                wbuf, wkey, kind, cbase, c, j = fill_units[fstate['u']]
                if fstate['u'] == nfirst and fstate['kc'] == 0:
                    dma('pool', wA[:], w_in_v[:, :, 1024:1536], [], ['wA'])
                b = 4 + (fstate['u'] % 2)
                kc = fstate['kc']
                mm(ps[b][:], wbuf[:, kc, c * P:(c + 1) * P], xT[:, kc, j * 512:(j + 1) * 512], kc == 0, kc == 7,
                   [wkey, ('xT', j)], [('ps', b)])
                fstate['kc'] += 1
                if fstate['kc'] == 8:
                    sl = slice(j * 512, (j + 1) * 512)
                    if kind == 'qc':
                        copy_evac(qc[:, c, sl], ps[b][:], [('ps', b)], [('qc', c, j), 'wC', 'wD'])
                    else:
                        k = fstate['n'] % 2
                        fstate['n'] += 1
                        act(thb[:, k, :], ps[b][:], AF.Tanh, [('ps', b)], [('thb', k)], scale=0.5)
                        S.add('dve', lambda e, k=k, b=b, cc=cbase + c, sl=sl: e.scalar_tensor_tensor(
                            cat[:, cc, sl], thb[:, k, :], 1.0, ps[b][:], ALU.add, ALU.mult),
                            [('thb', k), ('ps', b)], [('cat', cbase + c, j)])
                    fstate['kc'] = 0
                    fstate['u'] += 1

        dma('pool', wA[:], w_in_v[:, :, 4096:4608], [], ['wA'])
        dma('pool', wB[:], w_in_v[:, :, 4608:5120], [], ['wB'])

        def bankof(idx):
            return idx % 4

        def scores(idx):
            kind, h, T, n, half = items[idx]
            b = bankof(idx)
            if kind == 'w':
                kc_ = 2 * n + half
                qs = slice(T * 512, (T + 1) * 512)
                selw = (n < 2 * T and T >= 2)
                comp = (n == 2 * T)
                mm(ps[b][:], ka[:, h, kc_ * P:(kc_ + 1) * P], qa[:, h, qs], True, not (selw or comp),
                   [('ka', h, kc_ // 4), ('qa', h, T)], [('ps', b)])
                if selw:
                    mm(ps[b][:], esel[0:8, n, :], selT[0:8, h, (T - 2) * 512:(T - 1) * 512], False, True, ['esel', ('selT', h)], [('ps', b)])
                if comp:
                    mm(ps[b][:, 0:256], ident[:], triAB[:, half * 256:(half + 1) * 256], False, T < 2, ['ident', 'triAB'], [('ps', b)])
                    if T >= 2:
                        mm(ps[b][:, 256:512], esel[0:8, n, :], selT[0:8, h, (T - 2) * 512 + 256:(T - 1) * 512], False, True,
                           ['esel', ('selT', h)], [('ps', b)])
            else:
                jq = 2 * T + 1
                for hf in range(2):
                    kc_ = 2 * n + hf
                    hs = slice(hf * 256, (hf + 1) * 256)
                    mm(ps[b][:, hs], ka[:, h, kc_ * P:(kc_ + 1) * P], qa[:, h, jq * 256:(jq + 1) * 256], True, False,
                       [('ka', h, kc_ // 4), ('qa', h, T)], [('ps', b)])
                    mm(ps[b][:, hs], ident[:], triAB[:, hs], False, True, ['ident', 'triAB'], [('ps', b)])

        LA = 2
        fin = 0
        dstate = {'bank': 7}
        nsc = [0]

        def scores_upto(k):
            while nsc[0] <= min(k, len(items) - 1):
                scores(nsc[0])
                nsc[0] += 1

        selection(75)
        fill(48)
        selection(10 ** 6)
        scores_upto(2)
        for idx, (kind, h, T, n, half) in enumerate(items):
            scores_upto(idx + 3)
            fill(1)
            b = bankof(idx)
            ke = idx % 4
            fi = fin % 2
            first = (n == 0 and half == 0)
            act(Ebuf[:, ke, :], ps[b][:], AF.Exp, [('ps', b)], [('E', ke)], scale=SC128)
            if first:
                dstate['bank'] = 7
            dbk = dstate['bank']
            if kind == 'w':
                kc_ = 2 * n + half
                mm(ps[6][:], va[:, kc_, h * P:(h + 1) * P], Ebuf[:, ke, :], first, False, [('va', kc_), ('E', ke)], [('ps', 6)])
                mm(ps[dbk][:], onesb[:], Ebuf[:, ke, :], first, False, ['onesb', ('E', ke)], [('ps', dbk)])
            else:
                for hf in range(2):
                    kc_ = 2 * n + hf
                    hs = slice(hf * 256, (hf + 1) * 256)
                    mm(ps[6][:, 256:512], va[:, kc_, h * P:(h + 1) * P], Ebuf[:, ke, hs], False, hf == 1, [('va', kc_), ('E', ke)], [('ps', 6)])
                    mm(ps[dbk][:, 256:512], onesb[:], Ebuf[:, ke, hs], False, hf == 1, ['onesb', ('E', ke)], [('ps', dbk)])
                fin += 1
                qs = slice(T * 512, (T + 1) * 512)
                dstc = cat[:, 8 + h, qs]
                ck = ('cat', 8 + h, T)
                act(rden2[:, fi, :], ps[dbk][:], AF.Ln, [('ps', dbk)], [('rden2', fi)])
                S.add('act', lambda e, fi=fi: e.copy(osb[:, fi, :], ps[6][:]), [('ps', 6)], [('osb', fi)])
                act(rden2[:, fi, :], rden2[:, fi, :], AF.Exp, [('rden2', fi)], [('rden2', fi)], scale=-1.0)
                S.add('dve', lambda e, fi=fi, dstc=dstc: e.tensor_tensor(t2c[:, fi, :], rden2[:, fi, :], dstc, ALU.mult),
                      [('rden2', fi), ck], [('t2c', fi)])
                S.add('dve', lambda e, fi=fi, dstc=dstc: e.tensor_tensor(dstc, osb[:, fi, :], t2c[:, fi, :], ALU.mult),
                      [('osb', fi), ('t2c', fi)], [ck])
        fill(10 ** 6)
        S.barrier()

        o = OFF_R
        wD = view(o, [P, 8, 512]); o += 8192
        wC = view(o, [P, 8, 512]); o += 8192
        o += 16384
        wA3 = view(OFF_XM + 24620, [P, 8, 512])
        memT = view(o, [P, 8, 256]); o += 4096
        kmT = view(o, [P, 4, 256]); o += 2048
        vm = view(o, [P, 2, 512]); o += 2048
        Eb = view(o, [P, 2, 2, 512]); o += 4096
        rden = view(o, [P, 2, 512], F32); o += 4096
        t2b = view(o, [P, 2, 512], F32); o += 4096
        assert o <= ARENA_BYTES
        dma('pool', wD[:], w_kv_v[:, :, 0:512], [], ['wD'])
        dma('pool', memT[:], memT_v, [], ['memT'])
        dma('pool', wA3[:], w_in_v[:, :, 0:512], [], ['wA3'])
        wV = view(OFF_XM + 16416, [P, 8, 512])
        dma('pool', wV[:], w_kv_v[:, :, 512:1024], [], ['wV'])
        wB3 = view(OFF_R + 57344, [P, 8, 512])
        dma('pool', wB3[:], w_in_v[:, :, 512:1024], [], ['wB3'])
        inproj_fm(wC, 'wC', ev_silu(12))
        for h in range(4):
            b = next_bank(0, 4)
            for kc in range(8):
                mm(ps[b][:, 0:256], wD[:, kc, h * P:(h + 1) * P], memT[:, kc, :], kc == 0, kc == 7, ['wD', 'memT'], [('ps', b)])
            copy_evac(kmT[:, h, :], ps[b][:, 0:256], [('ps', b)], ['kmT'])
        for mc in range(2):
            b = next_bank(0, 4)
            for kc in range(8):
                mm(ps[b][:], memT[:, kc, mc * P:(mc + 1) * P], wV[:, kc, :], kc == 0, kc == 7, ['wV', 'memT'], [('ps', b)])
            copy_evac(vm[:, mc, :], ps[b][:], [('ps', b)], ['vm'])
        hj = [(h, j) for h in range(4) for j in range(4)]

        def qk1(it):
            h, j = hj[it]
            bS = 0 + 2 * (it % 2)
            for mc in range(2):
                mm(ps[bS + mc][:], kmT[:, h, mc * P:(mc + 1) * P], qc[:, h, j * 512:(j + 1) * 512], True, True,
                   ['kmT', ('qc', h, j)], [('ps', bS + mc)])

        def fin1(it):
            h, j = hj[it]
            buf = it % 2
            bo = 4 + (it % 2)
            bd = 6
            act(rden[:, buf, :], ps[bd][:], AF.Ln, [('ps', bd)], [('rden', buf)])
            act(rden[:, buf, :], rden[:, buf, :], AF.Exp, [('rden', buf)], [('rden', buf)], scale=-1.0)
            dstc = cat[:, 12 + h, j * 512:(j + 1) * 512]
            S.add('dve', lambda e: e.tensor_tensor(t2b[:, buf, :], rden[:, buf, :], dstc, ALU.mult),
                  [('rden', buf), ('cat', 12 + h, j)], [('t2b', buf)])
            S.add('dve', lambda e: e.tensor_tensor(dstc, ps[bo][:], t2b[:, buf, :], ALU.mult),
                  [('ps', bo), ('t2b', buf)], [('cat', 12 + h, j)])

        xm1 = view(OFF_XM, [P, 8, 2051])

        def fill1(it):
            c, j = it % 4, it // 4
            for kc in range(8):
                mm(ps[7][:], wA3[:, kc, c * P:(c + 1) * P], xT[:, kc, j * 512:(j + 1) * 512], kc == 0, kc == 7,
                   ['wA3', ('xT', j)], [('ps', 7)])
            S.add('dve', lambda e: e.tensor_copy(xm1[:, c, 3 + j * 512:3 + (j + 1) * 512], ps[7][:]), [('ps', 7)], [('xm', c, j)])

        qk1(0)
        for it, (h, j) in enumerate(hj):
            if True:
                if it + 1 < len(hj):
                    qk1(it + 1)
                fill1(it)
                buf = it % 2
                bS = 0 + 2 * (it % 2)
                bo = 4 + (it % 2)
                bd = 6
                for mc in range(2):
                    act(Eb[:, buf, mc, :], ps[bS + mc][:], AF.Exp, [('ps', bS + mc)], [('Eb', buf, mc)], scale=SC128)
                for mc in range(2):
                    mm(ps[bo][:], vm[:, mc, h * P:(h + 1) * P], Eb[:, buf, mc, :], mc == 0, mc == 1,
                       ['vm', ('Eb', buf, mc)], [('ps', bo)])
                if it > 0:
                    fin1(it - 1)
                for mc in range(2):
                    mm(ps[bd][:], onesb[:], Eb[:, buf, mc, :], mc == 0, mc == 1, ['onesb', ('Eb', buf, mc)], [('ps', bd)])
        fin1(len(hj) - 1)
        S.barrier()

        o = OFF_R
        wA = view(o, [P, 8, 512])
        wB = view(o + 8192, [P, 8, 512])
        wqT32 = view(o, [P, 8, P], F32)
        wkT32 = view(o + 4096, [P, 8, P], F32)
        wvT32 = view(o + 8192, [P, 8, P], F32)
        gates_sb = view(o, [8, NT], F32)
        lf_sb = view(o + 8192, [8, NT], F32)
        qTh = view(o, [P, 2, NT])
        kTh = view(o + 8192, [P, 2, NT]); o += 16384
        khat_tm = view(o, [P, 16, 256])
        wg32 = view(o, [P, 24, 8], F32)
        gT = view(o + 768, [P, 16, 16], F32)
        Icont = view(o + 1792, [P, 64], F32)
        Acont = view(o + 2048, [P, 64], F32)
        tmp1 = view(o + 2304, [P, 64], F32)
        tmp2 = view(o + 2560, [P, 64], F32); o += 8192
        vaug = view(o, [P, 16, 258]); o += 8256
        tot32 = view(o, [P, 16, 257], F32)
        diagw = view(o, [P, 8, 4, P]); o += 16448
        wqb = view(o, [P, 8, P]); o += 2048
        wkb = view(o, [P, 8, P]); o += 2048
        wvb = view(o, [P, 8, P]); o += 2048
        G1b = view(o, [P, 8, 8]); o += 128
        G2b = view(o, [P, 8, 8]); o += 128
        w_s = view(o, [P, 64], F32); o += 256
        khat = view(o, [P, 64], F32); o += 256
        einv = view(o, [P, 64], F32); o += 256
        gdec = view(o, [P, 64], F32); o += 256
        C32 = view(o, [P, 2, 258], F32); o += 2064
        Cbf = view(o, [P, 2, 2, 258]); o += 2064
        Sm = view(o, [P, 2, P]); o += 512
        hnb = view(o, [P, 2, 4, 256]); o += 4096
        tg = view(o, [P, 2, 512], F32); o += 4096
        st6 = view(o, [P, 16, 6], F32); o += 384
        mv = view(o, [P, 16, 2], F32); o += 128
        sm = [view(o + 64 * q_, [P, 16], F32) for q_ in range(8)]; o += 512
        assert o <= ARENA_BYTES, o
        xm = view(OFF_XM, [P, 8, 2051])
        xc = view(OFF_XT, [P, 8, NT])
        wo = view(OFF_XM, [P, 16, 1024])

        wA3 = view(OFF_XM + 24620, [P, 8, 512])
        S.add('pool', lambda e: e.memset(xm[:, 0:6, 0:3], 0.0), [], ['xmpad'])
        wB3 = view(OFF_R + 57344, [P, 8, 512])
        dma('pool', wA[:], w_in_v[:, :, 1024:1536], [], ['wA'])
        dma('pool', wB[:], w_in_v[:, :, 1536:2048], [], ['wB'])
        dma('pool', wqb[:], wq_d, [], ['wqb'])
        dma('pool', wkb[:], wk_d, [], ['wkb'])
        dma('pool', wvb[:], wv_d, [], ['wvb'])

        def ev_xm(cbase):
            def f(c, j, pst, pkey):
                extra = ['wA3'] if cbase + c >= 6 else []
                copy_evac(xm[:, cbase + c, 3 + j * 512:3 + (j + 1) * 512], pst[:], [pkey], [('xm', cbase + c, j)] + extra)
            return f

        S.add('pool', lambda e: e.memset(xm[:, 6:8, 0:3], 0.0), [], ['xmpad2', 'wA3'])
        inproj_fm(wB3, 'wB3', ev_xm(4))
        inproj_fm(wA, 'wA', ev_silu(0))
        inproj_fm(wB, 'wB', ev_silu(4))
        S.barrier()

        dma('sp', wqT32[:], wqT_d, [], ['wT32'])
        dma('sp', wkT32[:], wkT_d, [], ['wT32'])
        dma('sp', wvT32[:], wvT_d, [], ['wT32'])
        dma('sp', wg32[:], wg_d, [], ['wg32'])
        for c in range(8):
            for tap in range(4):
                S.add('dve', lambda e, c=c, tap=tap: e.tensor_scalar(diagw[:, c, tap, :], ident[:], convw[:, c, tap:tap + 1], None, ALU.mult),
                      ['ident', 'convw'], [('dw', c)])
        for c in range(8):
            for j in range(4):
                b = next_bank(0, 4)
                rd = [('dw', c), ('xm', c, j), 'xmpad', 'xmpad2'] + ([('xm', c, j - 1)] if j > 0 else [])
                for tap in range(4):
                    mm(ps[b][:], diagw[:, c, tap, :], xm[:, c, j * 512 + tap:j * 512 + tap + 512], tap == 0, tap == 3, rd, [('ps', b)])
                act(xc[:, c, j * 512:(j + 1) * 512], ps[b][:], AF.Silu, [('ps', b), 'convb'], [('xc', c, j)], bias=convb[:, c:c + 1])

        for c in range(8):
            b = 4 + (c % 2)
            mm(ps[b][:, 0:8], wqT32[:, c, :], wg32[:, c, :], True, False, ['wT32', 'wg32'], [('ps', b)])
            mm(ps[b][:, 0:8], wkT32[:, c, :], wg32[:, 8 + c, :], False, True, ['wT32', 'wg32'], [('ps', b)])
            mm(ps[b][:, 8:16], wvT32[:, c, :], wg32[:, 16 + c, :], True, True, ['wT32', 'wg32'], [('ps', b)])
            copy_evac(G1b[:, c, :], ps[b][:, 0:8], [('ps', b)], ['G1b'])
            copy_evac(G2b[:, c, :], ps[b][:, 8:16], [('ps', b)], ['G2b'])
        for j in range(4):
            sl = slice(j * 512, (j + 1) * 512)
            for c in range(8):
                mm(ps[7][0:8, :], G1b[:, c, :], xc[:, c, sl], c == 0, False, ['G1b', ('xc', c, j)], [('ps', 7)])
                mm(ps[7][0:8, :], G2b[:, c, :], xm[:, c, 3 + j * 512:3 + (j + 1) * 512], False, c == 7, ['G2b', ('xm', c, j)], [('ps', 7)])
            act(gates_sb[0:8, sl], ps[7][0:8, :], AF.Identity, [('ps', 7), 'bg'], [('gates', j), 'wT32'], bias=bg[:, 0:1])
        gk = [('gates', j) for j in range(4)]
        act(lf_sb[0:8, :], gates_sb[0:8, :], AF.Exp, gk, ['lf', 'wT32'], scale=-1.0)
        act(lf_sb[0:8, :], lf_sb[0:8, :], AF.Ln, ['lf', 'one1'], ['lf'], bias=one1[0:8, 0:1])
        for i in range(16):
            b = 4 + (i % 3)
            S.add('pe', lambda e, i=i, b=b: e.transpose(ps[b][:, 0:8], gates_sb[0:8, i * P:(i + 1) * P], ident32[0:8, 0:8]), gk + ['ident32'], [('ps', b)])
            S.add('pe', lambda e, i=i, b=b: e.transpose(ps[b][:, 8:16], lf_sb[0:8, i * P:(i + 1) * P], ident32[0:8, 0:8]), ['lf', 'ident32'], [('ps', b)])
            copy_evac(gT[:, i, :], ps[b][:, 0:16], [('ps', b)], ['gT'])
        S.add('dve', lambda e: e.tensor_copy(Icont.rearrange("p (i h) -> p i h", h=4), gT[:, :, 0:4]), ['gT'], ['Icont'])
        S.add('dve', lambda e: e.tensor_copy(Acont.rearrange("p (i h) -> p i h", h=4), gT[:, :, 12:16]), ['gT'], ['Acont'])
        mm(ps[0][:, 0:64], U32[:], Acont, True, True, ['U32', 'Acont'], [('ps', 0)])
        mm(ps[1][:, 0:64], ones32[:], Acont, True, True, ['ones32', 'Acont'], [('ps', 1)])
        S.add('dve', lambda e: e.tensor_tensor(tmp1, Icont, ps[0][:, 0:64], ALU.add), ['Icont', ('ps', 0)], ['tmp1'])
        S.add('dve', lambda e: e.tensor_tensor(tmp2, tmp1, ps[1][:, 0:64], ALU.subtract), ['tmp1', ('ps', 1)], ['tmp2'])
        act(w_s, tmp1, AF.Exp, ['tmp1'], ['w_s'])
        act(khat, tmp2, AF.Exp, ['tmp2'], ['khat'])
        act(einv, ps[0][:, 0:64], AF.Exp, [('ps', 0)], ['einv'])
        act(gdec, ps[1][:, 0:64], AF.Exp, [('ps', 1)], ['gdec'], scale=-1.0)
        S.add('dve', lambda e: e.tensor_scalar(w_s, w_s, SC256, None, ALU.mult), ['w_s'], ['w_s'])
        S.add('dve', lambda e: e.tensor_scalar(khat, khat, SC256, None, ALU.mult), ['khat'], ['khat'])
        S.barrier()

        S.add('pool', lambda e: e.memset(vaug[:, :, 256:258], 1.0), [], ['vones'])
        allxm = [('xm', c, j) for c in range(8) for j in range(4)]
        absd, mcl, v2, rstd, nmr = sm[0:5]

        def setup(h):
            c0 = 2 * h
            for dc in range(2):
                c = c0 + dc
                for j in range(4):
                    sl = slice(j * 512, (j + 1) * 512)
                    b = next_bank(0, 4)
                    mm(ps[b][:], wqb[:, c, :], xc[:, c, sl], True, True, ['wqb', ('xc', c, j)], [('ps', b)])
                    copy_evac(qTh[:, dc, sl], ps[b][:], [('ps', b)], [('qTh', j)])
                    b = next_bank(0, 4)
                    mm(ps[b][:], wkb[:, c, :], xc[:, c, sl], True, True, ['wkb', ('xc', c, j)], [('ps', b)])
                    copy_evac(kTh[:, dc, sl], ps[b][:], [('ps', b)], [('kTh', j)])
            for i in range(16):
                col = i * 4 + h
                b = 4 + (i % 4)
                for dc in range(2):
                    mm(ps[b][:, dc * P:(dc + 1) * P], xc[:, c0 + dc, i * P:(i + 1) * P], wkb[:, c0 + dc, :], True, True,
                       ['wkb', ('xc', c0 + dc, i // 4)], [('ps', b)])
                for dc in range(2):
                    mm(ps[b][:, 256 + dc * P:256 + (dc + 1) * P], xm[:, c0 + dc, 3 + i * P:3 + (i + 1) * P], wvb[:, c0 + dc, :], True, True,
                       ['wvb', ('xm', c0 + dc, i // 4)], [('ps', b)])
                if i % 2 == 0:
                    S.add('dve', lambda e, i=i, b=b, col=col: e.tensor_scalar(khat_tm[:, i, :], ps[b][:, 0:256], khat[:, col:col + 1], None, ALU.mult),
                          [('ps', b), 'khat'], [('khat_tm', i)])
                    S.add('act', lambda e, i=i, b=b: e.copy(vaug[:, i, 0:256], ps[b][:, 256:512]), [('ps', b)], [('vaug', i)])
                else:
                    act(khat_tm[:, i, :], ps[b][:, 0:256], AF.Copy, [('ps', b), 'khat'], [('khat_tm', i)], scale=khat[:, col:col + 1])
                    S.add('dve', lambda e, i=i, b=b: e.tensor_copy(vaug[:, i, 0:256], ps[b][:, 256:512]), [('ps', b)], [('vaug', i)])
            if h == 3:
                for q4 in range(4):
                    dma('pool', wo[:, 4 * q4:4 * q4 + 4, :], w_out_v[:, 4 * q4:4 * q4 + 4, :], [], [('wo', q4)] + (allxm if q4 == 0 else []))
            for dc in range(2):
                c = c0 + dc
                S.add('dve', lambda e, c=c: e.tensor_scalar(xc[:, c, :], xc[:, c, :], skipv[:, c:c + 1], None, ALU.mult),
                      [('xc', c, j) for j in range(4)] + ['skipv'], [('xc', c, j) for j in range(4)])

        def p1_front(h, i):
            tsl = slice(i * P, (i + 1) * P)
            so = (i % 2) * P
            for dc in range(2):
                mm(ps[0][:, so:so + P], kTh[:, dc, tsl], qTh[:, dc, tsl], dc == 0, dc == 1, [('kTh', i // 4), ('qTh', i // 4)], [('ps', 0)])
            mm(ps[1][:], ident[:], triAB[:], True, True, ['ident', 'triAB'], [('ps', 1)])
            if i < 15:
                for dc in range(2):
                    bD = 4 + 2 * (i % 2) + dc
                    mm(ps[bD][:, 0:257], khat_tm[:, i, dc * P:(dc + 1) * P], vaug[:, i, 0:257], True, True,
                       [('khat_tm', i), ('vaug', i), 'vones'], [('ps', bD)])

        def p1_rest(h, i):
            col = i * 4 + h
            tsl = slice(i * P, (i + 1) * P)
            k = i % 2
            so = (i % 2) * P
            S.add('dve', lambda e: e.scalar_tensor_tensor(Sm[:, k, :], ps[0][:, so:so + P], w_s[:, col:col + 1], triT[:], ALU.mult, ALU.mult),
                  [('ps', 0), 'w_s', 'triT'], [('Sm', k)])
            if i < 15:
                for dc in range(2):
                    bD = 4 + 2 * (i % 2) + dc
                    if i == 0:
                        S.add('dve', lambda e, dc=dc, bD=bD: e.tensor_copy(C32[:, dc, 0:257], ps[bD][:, 0:257]), [('ps', bD)], [('C32', dc)])
                    else:
                        S.add('dve', lambda e, dc=dc, bD=bD: e.scalar_tensor_tensor(
                            C32[:, dc, 0:257], C32[:, dc, 0:257], gdec[:, col:col + 1], ps[bD][:, 0:257], ALU.mult, ALU.add),
                            [('C32', dc), ('ps', bD), 'gdec'], [('C32', dc)])
                    S.add('act', lambda e, dc=dc: e.copy(Cbf[:, i % 2, dc, 0:257], C32[:, dc, 0:257]), [('C32', dc)], [('Cbf', i % 2, dc)])
            mm(ps[2][:, 0:257], Sm[:, k, :], vaug[:, i, 0:257], True, i == 0, [('Sm', k), ('vaug', i), 'vones'], [('ps', 2)])
            if i > 0:
                par = (i - 1) % 2
                for dc in range(2):
                    mm(ps[2][:, 0:257], qTh[:, dc, tsl], Cbf[:, par, dc, 0:257], False, dc == 1,
                       [('qTh', i // 4), ('Cbf', par, dc)], [('ps', 2)])
            S.add('act', lambda e: e.copy(tot32[:, i, :], ps[2][:, 0:257]), [('ps', 2)], [('tot', i)])

        def p1_stats(i):
            S.add('dve', lambda e: e.bn_stats(st6[:, i, :], tot32[:, i, 0:256]), [('tot', i)], [('st6', i)])
            S.add('dve', lambda e: e.bn_aggr(mv[:, i, :], st6[:, i, :]), [('st6', i)], ['mv'])

        def batch(h):
            einv_h = einv.rearrange("p (i h) -> p i h", h=4)[:, :, h]
            dn_v = tot32[:, :, 256]
            mean_v = mv[:, :, 0]
            var_v = mv[:, :, 1]
            tk_ = [('tot', i) for i in range(16)]
            V = 'dve'
            S.add(V, lambda e: e.scalar_tensor_tensor(absd, dn_v, -1.0, dn_v, ALU.mult, ALU.max), tk_, ['absd'])
            S.add(V, lambda e: e.tensor_tensor(mcl, absd, einv_h, ALU.max), ['absd', 'einv'], ['mcl'])
            S.add(V, lambda e: e.tensor_tensor(v2, mcl, mcl, ALU.mult), ['mcl'], ['v2'])
            S.add(V, lambda e: e.scalar_tensor_tensor(v2, v2, LN_EPS, var_v, ALU.mult, ALU.add), ['v2', 'mv'], ['v2'])
            act(v2, v2, AF.Sqrt, ['v2'], ['v2'])
            S.add(V, lambda e: e.reciprocal(rstd, v2), ['v2'], ['rstd'])
            S.add(V, lambda e: e.scalar_tensor_tensor(nmr, mean_v, -1.0, rstd, ALU.mult, ALU.mult), ['mv', 'rstd'], ['nmr'])

        def pass2_group(h, g4):
            c0 = 2 * h
            kb = g4 % 2
            gsl = slice(g4 * 512, (g4 + 1) * 512)
            pTb = ps[3][:].bitcast(BF16)
            for ci in range(4):
                i = 4 * g4 + ci
                act(hnb[:, kb, ci, :], tot32[:, i, 0:256], AF.Identity, [('tot', i), 'rstd', 'nmr'], [('hnb', kb, ci)],
                    scale=rstd[:, i:i + 1], bias=nmr[:, i:i + 1])
                for ec in range(2):
                    sl_ = slice((ci * 2 + ec) * P, (ci * 2 + ec + 1) * P)
                    S.add('pe', lambda e, ci=ci, ec=ec, sl_=sl_: e.transpose(pTb[:, sl_], hnb[:, kb, ci, ec * P:(ec + 1) * P], ident[:]),
                          [('hnb', kb, ci), 'ident'], [('ps', 3)])
            for ec in range(2):
                c = c0 + ec
                pv = pTb.rearrange("p (ci ec t) -> p ci ec t", ci=4, ec=2)[:, :, ec, :]
                S.add('dve', lambda e, ec=ec, c=c, pv=pv: e.scalar_tensor_tensor(
                    tg[:, ec, :].rearrange("p (ci t) -> p ci t", ci=4), pv, normg[:, c:c + 1],
                    xc[:, c, gsl].rearrange("p (ci t) -> p ci t", ci=4), ALU.mult, ALU.add),
                    [('ps', 3), 'normg', ('xc', c, g4)], [('tg', ec)])
                S.add('pool', lambda e, ec=ec, c=c: e.tensor_tensor(cat[:, c, gsl], tg[:, ec, :], cat[:, c, gsl], ALU.mult),
                      [('tg', ec), ('cat', c, g4)], [('cat', c, g4)])

        for h in range(4):
            setup(h)
            p1_front(h, 0)
            for i in range(16):
                if h > 0 and i % 4 == 0:
                    pass2_group(h - 1, i // 4)
                if i + 1 < 16:
                    p1_front(h, i + 1)
                p1_rest(h, i)
                if i > 0:
                    p1_stats(i - 1)
            p1_stats(15)
            batch(h)
        for g4 in range(4):
            pass2_group(3, g4)
        S.barrier()

        if debug:
            dma('sp', dbg_cat, cat, [], ['dbg'])
        o = OFF_R
        lngb = view(o, [P, DM], F32); o += 4096
        lnbb = view(o, [P, DM], F32); o += 4096
        xrow = view(o, [P, 3, DM], F32); o += 12288
        pre = view(o, [P, 2, DM], F32); o += 8192
        yb = view(o, [P, 2, DM], F32); o += 8192
        st4 = view(o, [P, 2, 12], F32); o += 96
        mv4 = view(o, [P, 2, 2], F32); o += 16
        sc4 = view(o, [P, 2, 4], F32); o += 32
        dma('sp', lngb, lng_d.partition_broadcast(P), [], ['lngb'])
        dma('sp', lnbb, lnb_d.partition_broadcast(P), [], ['lnbb'])
        for i in range(2):
            dma('sp', xrow[:, i % 3, :], x_d[i * P:(i + 1) * P, :], [], [('xrow', i % 3)])
        for i in range(16):
            k = i % 2
            kx = i % 3
            tsl = slice(i * P, (i + 1) * P)
            if i + 2 < 16:
                dma('sp', xrow[:, (i + 2) % 3, :], x_d[(i + 2) * P:(i + 3) * P, :], [], [('xrow', (i + 2) % 3)])
            for half in range(2):
                b = next_bank(0, 4)
                hs = slice(half * 512, (half + 1) * 512)
                for kc in range(16):
                    mm(ps[b][:], cat[:, kc, tsl], wo[:, kc, hs], kc == 0, kc == 15, [('wo', kc // 4)], [('ps', b)])
                S.add('dve', lambda e, k=k, kx=kx, b=b, hs=hs: e.scalar_tensor_tensor(pre[:, k, hs], xrow[:, kx, hs], ALPHA, ps[b][:], ALU.mult, ALU.add),
                      [('xrow', kx), ('ps', b)], [('pre', k, half)])
                S.add('dve', lambda e, k=k, half=half, hs=hs: e.bn_stats(st4[:, k, 6 * half:6 * half + 6], pre[:, k, hs]), [('pre', k, half)], [('st4', k, half)])
            S.add('dve', lambda e, k=k: e.bn_aggr(mv4[:, k, :], st4[:, k, :]), [('st4', k, 0), ('st4', k, 1)], [('mv4', k)])
            S.add('dve', lambda e, k=k: e.tensor_scalar(sc4[:, k, 0:1], mv4[:, k, 1:2], LN_EPS, None, ALU.add), [('mv4', k)], [('sc4a', k)])
            act(sc4[:, k, 1:2], sc4[:, k, 0:1], AF.Sqrt, [('sc4a', k)], [('sc4b', k)])
            S.add('dve', lambda e, k=k: e.reciprocal(sc4[:, k, 2:3], sc4[:, k, 1:2]), [('sc4b', k)], [('sc4c', k)])
            S.add('dve', lambda e, k=k: e.scalar_tensor_tensor(sc4[:, k, 3:4], mv4[:, k, 0:1], -1.0, sc4[:, k, 2:3], ALU.mult, ALU.mult),
                  [('mv4', k), ('sc4c', k)], [('sc4d', k)])
            act(yb[:, k, :], pre[:, k, :], AF.Identity, [('pre', k, 0), ('pre', k, 1), ('sc4c', k), ('sc4d', k)], [('yb', k)],
                scale=sc4[:, k, 2:3], bias=sc4[:, k, 3:4])
            S.add('pool', lambda e, k=k: e.tensor_tensor(yb[:, k, :], yb[:, k, :], lngb, ALU.mult), [('yb', k), 'lngb'], [('yb', k)])
            S.add('pool', lambda e, k=k: e.tensor_tensor(yb[:, k, :], yb[:, k, :], lnbb, ALU.add), [('yb', k), 'lnbb'], [('yb', k)])
            dma('sp', y_d[tsl, :], yb[:, k, :], [('yb', k)], [('y', i)])

        S.barrier()
        S.flush()
    return nc


def _prep_shared(inp):
    f = np.float32
    sh = {}
    sh["w_in"] = np.ascontiguousarray(inp["w_in"], dtype=f)
    sh["w_kv"] = np.ascontiguousarray(inp["w_mem_kv"], dtype=f)
    sh["w_out"] = np.ascontiguousarray(inp["w_out"], dtype=f)
    sh["convw"] = np.ascontiguousarray(np.asarray(inp["mlstm_conv_w"], dtype=f).T.reshape(8, P, 4).transpose(1, 0, 2))
    sh["convb"] = np.ascontiguousarray(np.asarray(inp["mlstm_conv_b"], dtype=f).reshape(8, P).T)

    def bd(w):
        w = np.asarray(w, dtype=f)
        full = np.zeros((8, P, P), dtype=f)
        for g in range(256):
            c, r = divmod(g * 4, P)
            full[c, r:r + 4, r:r + 4] = w[g]
        return np.ascontiguousarray(full.transpose(1, 0, 2))

    sh["wq_bd"] = bd(inp["mlstm_wq"])
    sh["wk_bd"] = bd(inp["mlstm_wk"])
    sh["wv_bd"] = bd(inp["mlstm_wv"])

    def bdT(w):
        w = np.asarray(w, dtype=f)
        full = np.zeros((8, P, P), dtype=f)
        for g in range(256):
            c, r = divmod(g * 4, P)
            full[c, r:r + 4, r:r + 4] = w[g].T
        return np.ascontiguousarray(full.transpose(1, 0, 2))

    sh["wqT_bd"] = bdT(inp["mlstm_wq"])
    sh["wkT_bd"] = bdT(inp["mlstm_wk"])
    sh["wvT_bd"] = bdT(inp["mlstm_wv"])
    sh["wg"] = np.ascontiguousarray(np.asarray(inp["mlstm_w_gates"], dtype=f).reshape(24, P, 8).transpose(1, 0, 2))
    sh["bg"] = np.ascontiguousarray(np.asarray(inp["mlstm_b_gates"], dtype=f).reshape(8, 1))
    sh["normg"] = np.ascontiguousarray(np.asarray(inp["mlstm_norm_g"], dtype=f).reshape(8, P).T)
    sh["skip"] = np.ascontiguousarray(np.asarray(inp["mlstm_skip"], dtype=f).reshape(8, P).T)
    sh["ln_g"] = np.ascontiguousarray(np.asarray(inp["ln_g"], dtype=f).reshape(1, DM))
    sh["ln_b"] = np.ascontiguousarray(np.asarray(inp["ln_b"], dtype=f).reshape(1, DM))
    half = 64
    inv_freq = (np.float32(10000.0) ** (-(np.arange(half, dtype=np.float32) * np.float32(2.0) / np.float32(128)))).astype(np.float32)
    invf = (inv_freq.astype(np.float64) / (2.0 * np.pi)).astype(f)
    sh["invf"] = np.ascontiguousarray(np.concatenate([invf, invf]).reshape(P, 1))
    return sh


def _in_maps(inp, cores):
    sh = _prep_shared(inp)
    x = np.asarray(inp["x"], dtype=np.float32)
    mem = np.asarray(inp["mem"], dtype=np.float32)
    pos = np.asarray(inp["positions"], dtype=np.int32)
    maps = []
    for b in cores:
        m = dict(sh)
        m["x"] = np.ascontiguousarray(x[b])
        m["xT"] = np.ascontiguousarray(x[b].T)
        m["memT"] = np.ascontiguousarray(mem[b].T)
        m["pos"] = np.ascontiguousarray(pos[b].reshape(1, NT))
        maps.append(m)
    return maps


def kernel(**inputs):
    nc = build_nc(debug=False)
    maps = _in_maps(inputs, list(range(8)))
    res = run_bass_kernel_spmd(nc, maps, core_ids=list(range(8)))
    return np.stack([np.asarray(r["y"], dtype=np.float32) for r in res.results], axis=0)
```

```python
import numpy as np
from contextlib import ExitStack
import concourse.bass as bass
import concourse.mybir as mybir
from concourse.bass_utils import run_bass_kernel_spmd

F32 = mybir.dt.float32
BF16 = mybir.dt.bfloat16
I32 = mybir.dt.int32
AF = mybir.ActivationFunctionType
ALU = mybir.AluOpType
AX = mybir.AxisListType

P = 128
NT = 2048
DM = 1024
INW = 5120
ALPHA = 2.0 ** 0.25
LN_EPS = 1e-5
SC128 = 128.0 ** -0.5
SC256 = 256.0 ** -0.5


class Sched:
    ENG = ['pe', 'act', 'dve', 'pool', 'sp']
    BLK = {'pe': 'tensor', 'act': 'scalar', 'dve': 'vector', 'pool': 'gpsimd', 'sp': 'sync'}

    def __init__(self, nc, es, nds=8):
        self.nc = nc
        self.sem = {}
        self.ccnt = {}
        for e in ['pe', 'act', 'dve', 'pool']:
            self.sem['c_' + e] = es.enter_context(nc.semaphore('c_' + e))
            self.ccnt[e] = 0
        self.nds = nds
        self.dcnt = {}
        self.drr = {}
        for q in ['sp', 'pool']:
            self.drr[q] = 0
            for i in range(nds):
                k = 'd_%s%d' % (q, i)
                self.sem[k] = es.enter_context(nc.semaphore(k))
                self.dcnt[k] = 0
        self.known = {e: {} for e in self.ENG}
        self.lastw = {}
        self.readers = {}
        self.ops = {e: [] for e in self.ENG}
        self.nops = 0

    def add(self, eng, fn, reads=(), writes=(), dma=False):
        waits = {}
        writes = list(writes) + [t for t in reads if isinstance(t, tuple) and t[0] == 'ps' and t not in writes]

        def need(tok, raw):
            k, v, teng, tdma = tok
            if not dma and not tdma and teng == eng and eng == 'pe':
                return
            if self.known[eng].get(k, 0) >= v:
                return
            if waits.get(k, 0) < v:
                waits[k] = v

        for t in reads:
            tok = self.lastw.get(t)
            if tok is not None:
                need(tok, True)
        for t in writes:
            tok = self.lastw.get(t)
            if tok is not None:
                need(tok, False)
            for tok in self.readers.get(t, ()):
                need(tok, False)
        if dma:
            i = self.drr[eng]
            self.drr[eng] = (i + 1) % self.nds
            k = 'd_%s%d' % (eng, i)
            prev = self.dcnt[k]
            if prev > 0 and self.known[eng].get(k, 0) < prev:
                waits[k] = max(waits.get(k, 0), prev)
            self.dcnt[k] = prev + 16
            tok = (k, prev + 16, eng, True)
            inc = (k, 16)
        else:
            self.ccnt[eng] += 1
            tok = ('c_' + eng, self.ccnt[eng], eng, False)
            inc = ('c_' + eng, 1)
        for k, v in waits.items():
            self.known[eng][k] = v
        for t in writes:
            self.lastw[t] = tok
            self.readers[t] = []
        for t in reads:
            lst = self.readers.setdefault(t, [])
            lst[:] = [x for x in lst if x[0] != tok[0]]
            lst.append(tok)
        self.ops[eng].append((list(waits.items()), fn, inc))
        self.nops += 1

    def barrier(self):
        allv = {}
        for e in ['pe', 'act', 'dve', 'pool']:
            if self.ccnt[e] > 0:
                allv['c_' + e] = self.ccnt[e]
        for k, v in self.dcnt.items():
            if v > 0:
                allv[k] = v
        for e in self.ENG:
            w = []
            for k, v in allv.items():
                if self.known[e].get(k, 0) < v:
                    w.append((k, v))
                    self.known[e][k] = v
            if w:
                self.ops[e].append((w, None, None))
        self.lastw = {}
        self.readers = {}

    def flush(self):
        sem = self.sem
        with self.nc.Block() as block:
            for e in self.ENG:
                ops = self.ops[e]
                if not ops:
                    continue

                def body(engine, ops=ops):
                    for waits, fn, inc in ops:
                        for k, v in waits:
                            engine.wait_ge(sem[k], v)
                        if fn is not None:
                            ins = fn(engine)
                            ins.then_inc(sem[inc[0]], inc[1])

                getattr(block, self.BLK[e])(body)
        self.ops = {e: [] for e in self.ENG}


def build_nc(debug=False):
    nc = bass.Bass("TRN2", target_bir_lowering=False)

    def din(name, shape, dt=F32):
        return nc.dram_tensor(name, list(shape), dt, kind="ExternalInput").ap()

    xT_d = din("xT", [DM, NT])
    x_d = din("x", [NT, DM])
    memT_d = din("memT", [DM, 256])
    pos_d = din("pos", [1, NT], I32)
    w_in_d = din("w_in", [DM, INW])
    w_kv_d = din("w_kv", [DM, 1024])
    w_out_d = din("w_out", [2048, DM])
    convw_d = din("convw", [P, 8, 4])
    convb_d = din("convb", [P, 8])
    wq_d = din("wq_bd", [P, 8, P])
    wk_d = din("wk_bd", [P, 8, P])
    wv_d = din("wv_bd", [P, 8, P])
    wg_d = din("wg", [P, 24, 8])
    wqT_d = din("wqT_bd", [P, 8, P])
    wkT_d = din("wkT_bd", [P, 8, P])
    wvT_d = din("wvT_bd", [P, 8, P])
    bg_d = din("bg", [8, 1])
    ng_d = din("normg", [P, 8])
    sk_d = din("skip", [P, 8])
    lng_d = din("ln_g", [1, DM])
    lnb_d = din("ln_b", [1, DM])
    invf_d = din("invf", [P, 1])
    y_d = nc.dram_tensor("y", [NT, DM], F32, kind="ExternalOutput").ap()
    if debug:
        dbg_cat = nc.dram_tensor("dbg_cat", [P, 16, NT], BF16, kind="ExternalOutput").ap()

    w_in_v = w_in_d.rearrange("(kc p) n -> p kc n", p=P)
    w_kv_v = w_kv_d.rearrange("(kc p) n -> p kc n", p=P)
    w_out_v = w_out_d.rearrange("(kc p) n -> p kc n", p=P)
    xT_v = xT_d.rearrange("(kc p) t -> p kc t", p=P)
    memT_v = memT_d.rearrange("(kc p) t -> p kc t", p=P)

    with ExitStack() as es:
        S = Sched(nc, es)
        ARENA_BYTES = 200 * 1024
        arena = es.enter_context(nc.sbuf_tensor("arena", [P, ARENA_BYTES // 2], BF16))
        ps = [es.enter_context(nc.psum_tensor("ps%d" % k, [P, 512], F32)) for k in range(8)]

        def view(off, shape, dt=BF16):
            esz = 2 if dt == BF16 else 4
            n = 1
            for s_ in shape[1:]:
                n *= s_
            nbytes = n * esz
            assert off % 4 == 0 and off + nbytes <= ARENA_BYTES, (off, nbytes)
            ap = arena[0:shape[0], off // 2: (off + nbytes) // 2]
            if dt != BF16:
                ap = ap.bitcast(dt)
            if len(shape) == 3:
                ap = ap.rearrange("p (a b) -> p a b", a=shape[1])
            elif len(shape) == 4:
                ap = ap.rearrange("p (a b c) -> p a b c", a=shape[1], b=shape[2])
            return ap

        OFF_CAT = 0
        OFF_XT = 65536
        OFF_XM = OFF_XT + 32768
        OFF_R = OFF_XM + 32816 + 16
        cat = view(OFF_CAT, [P, 16, NT])
        xT = view(OFF_XT, [P, 8, NT])

        def sb(name, shape, dt):
            return es.enter_context(nc.sbuf_tensor(name, shape, dt))

        ident = sb("ident", [P, P], BF16)
        ident32 = sb("ident32", [P, P], F32)
        onesb = sb("onesb", [P, P], BF16)
        ones32 = sb("ones32", [P, P], F32)
        U32 = sb("U32", [P, P], F32)
        triT = sb("triT", [P, P], BF16)
        triAB = sb("triAB", [P, 512], BF16)
        esel = sb("esel", [8, 8, P], BF16)
        one1 = sb("one1", [P, 1], F32)
        lnhalf = sb("lnhalf", [P, 1], F32)
        bg = sb("bg_s", [8, 1], F32)
        invf = sb("invf_s", [P, 1], F32)
        convw = sb("convw_s", [P, 8, 4], F32)
        convb = sb("convb_s", [P, 8], F32)
        normg = sb("normg_s", [P, 8], F32)
        skipv = sb("skip_s", [P, 8], F32)
        negm = sb("negm", [P, 4, 4, 2, 8], F32)

        cnt = [0]

        def uid():
            cnt[0] += 1
            return cnt[0]

        def mm(out, lhsT, rhs, start, stop, reads, writes):
            S.add('pe', lambda e: e.matmul(out, lhsT, rhs, start=start, stop=stop), reads, writes)

        def act(out, in_, func, reads, writes, eng='act', **kw):
            S.add(eng, lambda e: e.activation(out=out, in_=in_, func=func, **kw), reads, writes)

        def dma(q, out, in_, reads, writes):
            S.add(q, lambda e: e.dma_start(out=out, in_=in_), reads, writes, dma=True)

        dma('pool', view(OFF_CAT + 16384, [P, NT], F32), pos_d.partition_broadcast(P), [], ['tB'])
        dma('pool', xT[:, :, 0:512], xT_v[:, :, 0:512], [], [('xT', 0)])
        for dst, src, nm in [(bg, bg_d, 'bg'), (invf, invf_d, 'invf'), (convw, convw_d, 'convw'), (convb, convb_d, 'convb'),
                             (normg, ng_d, 'normg'), (skipv, sk_d, 'skipv')]:
            dma('sp', dst[:], src, [], [nm])
        def build_consts():
            G = 'pool'
            S.add(G, lambda e: e.memset(ident[:], 1.0), [], ['ident'])
            S.add(G, lambda e: e.affine_select(out=ident[:], in_=ident[:], pattern=[[1, P]], compare_op=ALU.is_equal, fill=0.0,
                                               base=0, channel_multiplier=-1), ['ident'], ['ident'])
            S.add(G, lambda e: e.memset(ident32[:], 1.0), [], ['ident32'])
            S.add(G, lambda e: e.affine_select(out=ident32[:], in_=ident32[:], pattern=[[1, P]], compare_op=ALU.is_equal, fill=0.0,
                                               base=0, channel_multiplier=-1), ['ident32'], ['ident32'])
            S.add(G, lambda e: e.memset(onesb[:], 1.0), [], ['onesb'])
            S.add(G, lambda e: e.memset(ones32[:], 1.0), [], ['ones32'])
            S.add(G, lambda e: e.memset(one1[:], 1.0), [], ['one1'])
            S.add(G, lambda e: e.memset(lnhalf[:], float(np.log(0.5))), [], ['lnhalf'])
            S.add(G, lambda e: e.memset(U32[:], 1.0), [], ['U32'])
            S.add(G, lambda e: e.affine_select(out=U32[:], in_=U32[:], pattern=[[1, P]], compare_op=ALU.is_ge, fill=0.0,
                                               base=0, channel_multiplier=-1), ['U32'], ['U32'])
            S.add(G, lambda e: e.memset(triT[:], 1.0), [], ['triT'])
            S.add(G, lambda e: e.affine_select(out=triT[:], in_=triT[:], pattern=[[1, P]], compare_op=ALU.is_ge, fill=0.0,
                                               base=0, channel_multiplier=-1), ['triT'], ['triT'])
            S.add(G, lambda e: e.memset(triAB[:], 1.0), [], ['triAB'])
            S.add(G, lambda e: e.affine_select(out=triAB[:, 0:256], in_=triAB[:, 0:256], pattern=[[1, 256]], compare_op=ALU.is_ge,
                                               fill=0.0, base=0, channel_multiplier=-1), ['triAB'], ['triAB'])
            S.add(G, lambda e: e.affine_select(out=triAB[:, 256:512], in_=triAB[:, 256:512], pattern=[[1, 256]], compare_op=ALU.is_ge,
                                               fill=0.0, base=-128, channel_multiplier=-1), ['triAB'], ['triAB'])
            S.add(G, lambda e: e.tensor_scalar(triAB[:], triAB[:], -1.0, 30000.0, ALU.add, ALU.mult), ['triAB'], ['triAB'])
            S.add(G, lambda e: e.memset(esel[:], 1.0), [], ['esel'])
            S.add(G, lambda e: e.affine_select(out=esel[:], in_=esel[:], pattern=[[1, 8], [0, P]], compare_op=ALU.is_equal, fill=0.0,
                                               base=0, channel_multiplier=-1), ['esel'], ['esel'])
            S.add(G, lambda e: e.memset(negm[:], 0.0), [], ['negm'])
            for j in range(4, 8):
                S.add(G, lambda e, j=j: e.memset(negm[:, :, j - 4, :, j:8], -1e30), ['negm'], ['negm'])


        bank_rr = [0]

        def next_bank(lo, hi):
            b = lo + (bank_rr[0] % (hi - lo))
            bank_rr[0] += 1
            return b

        def inproj_fm(wbuf, wkey, evac, nchunk=4, banks=(0, 4), jouter=False):
            order = [(c, j) for j in range(4) for c in range(nchunk)] if jouter else [(c, j) for c in range(nchunk) for j in range(4)]
            for c, j in order:
                b = next_bank(*banks)
                for kc in range(8):
                    mm(ps[b][:], wbuf[:, kc, c * P:(c + 1) * P], xT[:, kc, j * 512:(j + 1) * 512], kc == 0, kc == 7,
                       [wkey, ('xT', j)], [('ps', b)])
                evac(c, j, ps[b], ('ps', b))

        evac_rr = [0]

        def copy_evac(dst_ap, src_ap, reads, writes):
            evac_rr[0] += 1
            if evac_rr[0] % 2 == 0:
                S.add('act', lambda e: e.copy(dst_ap, src_ap), reads, writes)
            else:
                S.add('dve', lambda e: e.tensor_copy(dst_ap, src_ap), reads, writes)

        cosB = view(OFF_CAT, [P, NT], F32)
        sinS = view(OFF_CAT + 8192, [P, NT], F32)
        tB = view(OFF_CAT + 16384, [P, NT], F32)
        tC = view(OFF_CAT + 24576, [P, NT], F32)
        tA = view(OFF_XM, [P, NT], I32)
        V = 'dve'
        S.add(V, lambda e: e.tensor_scalar(tB, tB, invf[:, 0:1], None, ALU.mult), ['tB', 'invf'], ['tB'])
        S.add(V, lambda e: e.tensor_copy(tA, tB), ['tB'], ['tA'])
        S.add(V, lambda e: e.tensor_copy(tC, tA), ['tA'], ['tC'])
        S.add(V, lambda e: e.tensor_tensor(tC, tB, tC, ALU.subtract), ['tB', 'tC'], ['tC'])
        act(sinS[0:64, :], tC[0:64, :], AF.Sin, ['tC'], ['sinS0'], scale=-2.0 * np.pi)
        act(sinS[64:128, :], tC[64:128, :], AF.Sin, ['tC'], ['sinS1'], scale=2.0 * np.pi)
        S.add(V, lambda e: e.tensor_scalar(tB, tB, 0.25, None, ALU.add), ['tB'], ['tB'])
        S.add(V, lambda e: e.tensor_copy(tA, tB), ['tB'], ['tA'])
        S.add(V, lambda e: e.tensor_copy(tC, tA), ['tA'], ['tC'])
        S.add(V, lambda e: e.tensor_tensor(tC, tB, tC, ALU.subtract), ['tB', 'tC'], ['tC'])
        act(cosB, tC, AF.Sin, ['tC'], ['cosB'], scale=2.0 * np.pi)

        def ev_silu(cbase):
            def f(c, j, pst, pkey):
                dst = cat[:, cbase + c, j * 512:(j + 1) * 512]
                act(dst, pst[:], AF.Silu, [pkey], [('cat', cbase + c, j)])
            return f

        o = OFF_R
        wA = view(o, [P, 8, 512]); o += 8192
        wB = view(o, [P, 8, 512]); o += 8192
        wC = view(o, [P, 8, 512]); o += 8192
        wD = view(o, [P, 8, 512]); o += 8192
        va = view(o, [P, 16, 512]); o += 16384
        rt1 = view(o, [P, 2, 512], F32)
        selT = view(o, [8, 4, 1024])
        rt2 = view(o + 4096, [P, 2, 512], F32); o += 8192
        Ebuf = view(o, [P, 4, 512]); o += 4096
        Esum = view(o, [P, 2, 512], F32); o += 4096
        km32 = view(o, [P, 4, 8], F32); o += 128
        kmb = view(o, [P, 4, 8]); o += 64
        gsb = view(o, [P, 32, 8], F32); o += 1024
        top8 = view(o, [P, 32, 8], F32); o += 1024
        selb = view(o, [P, 32, 8]); o += 512
        thb = view(o, [P, 2, 512], F32)
        Ehl = view(o, [P, 4, 512]); o += 4096
        assert o <= ARENA_BYTES, o
        qc = view(OFF_R + 16384, [P, 4, NT])
        rden2 = view(OFF_CAT + 16384, [P, 2, 512], F32)
        t2c = view(OFF_CAT + 16384 + 4096, [P, 2, 512], F32)
        osb = view(OFF_CAT + 16384 + 8192, [P, 2, 512], F32)
        qa = view(OFF_XM, [P, 4, NT])
        ka = view(OFF_XM + 16384, [P, 4, NT])
        rtk = [('rt1', 0), ('rt1', 1), ('rt2', 0, 0), ('rt2', 0, 1), ('rt2', 1, 0), ('rt2', 1, 1)]

        dma('pool', wA[:], w_in_v[:, :, 2048:2560], [], ['wA'])
        dma('pool', wB[:], w_in_v[:, :, 3072:3584], [], ['wB'])
        dma('pool', wC[:], w_in_v[:, :, 2560:3072], [], ['wC'])
        dma('pool', wD[:], w_in_v[:, :, 3584:4096], [], ['wD'])
        for j in range(1, 4):
            dma('pool', xT[:, :, j * 512:(j + 1) * 512], xT_v[:, :, j * 512:(j + 1) * 512], [], [('xT', j)])
        build_consts()
        rr2 = [0]

        def ev_rot(dst, name):
            def f(c, j, pst, pkey):
                k = rr2[0] % 2
                rr2[0] += 1
                sl = slice(j * 512, (j + 1) * 512)
                S.add('dve', lambda e: e.tensor_tensor(rt1[:, k, :], pst[:], cosB[:, sl], ALU.mult), [pkey, 'cosB'], [('rt1', k)])
                S.add('dve', lambda e: e.tensor_tensor(rt2[0:64, k, :], pst[64:128, :], sinS[0:64, sl], ALU.mult), [pkey, 'sinS0'], [('rt2', k, 0)])
                S.add('dve', lambda e: e.tensor_tensor(rt2[64:128, k, :], pst[0:64, :], sinS[64:128, sl], ALU.mult), [pkey, 'sinS1'], [('rt2', k, 1)])
                S.add('pool', lambda e: e.tensor_tensor(dst[:, c, sl], rt1[:, k, :], rt2[:, k, :], ALU.add),
                      [('rt1', k), ('rt2', k, 0), ('rt2', k, 1)], [(name, c, j)])
            return f

        def fm_tile(wbuf, wkey, evac, c, j):
            b = next_bank(0, 4)
            for kc in range(8):
                mm(ps[b][:], wbuf[:, kc, c * P:(c + 1) * P], xT[:, kc, j * 512:(j + 1) * 512], kc == 0, kc == 7,
                   [wkey, ('xT', j)], [('ps', b)])
            evac(c, j, ps[b], ('ps', b))

        def va_tile(i):
            b = next_bank(0, 4)
            for kc in range(8):
                mm(ps[b][:], xT[:, kc, i * P:(i + 1) * P], wB[:, kc, :], kc == 0, kc == 7, ['wB', ('xT', i // 4)], [('ps', b)])
            S.add('act', lambda e, i=i, b=b: e.copy(va[:, i, :], ps[b][:]), [('ps', b)], [('va', i)])

        evq, evk, evz = ev_rot(qa, 'qa'), ev_rot(ka, 'ka'), ev_silu(8)
        for c in range(4):
            fm_tile(wA, 'wA', evq, c, 0)
        for c in range(4):
            va_tile(c)
        for c in range(4):
            fm_tile(wC, 'wC', evk, c, 0)
            fm_tile(wD, 'wD', evz, c, 0)
        for j in range(1, 4):
            for c in range(4):
                fm_tile(wA, 'wA', evq, c, j)
                va_tile(j * 4 + c)
                fm_tile(wC, 'wC', evk, c, j)
                fm_tile(wD, 'wD', evz, c, j)
        sel_ops = []
        for h in range(4):
            sel_ops.append(lambda h=h: S.add('dve', lambda e: e.tensor_reduce(km32[:, h, :], ka[:, h, :].rearrange("p (n k) -> p n k", n=8), AX.X, ALU.add),
                                             [('ka', h, j) for j in range(4)], [('km32', h)]))
            sel_ops.append(lambda h=h: S.add('dve', lambda e: e.tensor_scalar(kmb[:, h, :], km32[:, h, :], 1.0 / 256.0, None, ALU.mult), [('km32', h)], [('kmb', h)]))
        for h in range(4):
            for qi in range(8, 16):
                g_ = h * 8 + (qi - 8)
                sel_ops.append(lambda h=h, qi=qi, g_=g_: mm(ps[4][:, g_ * 8:(g_ + 1) * 8], qa[:, h, qi * P:(qi + 1) * P], kmb[:, h, :], True, True,
                                                            [('qa', h, qi // 4), ('kmb', h)], [('ps', 4)]))
        sel_ops.append(lambda: S.add('dve', lambda e: e.tensor_tensor(gsb[:], ps[4][:, 0:256].rearrange("p (g c) -> p g c", c=8),
                                                                      negm[:].rearrange("p h j r c -> p (h j r) c"), ALU.add), [('ps', 4), 'negm'], ['gsb']))
        for g_ in range(32):
            sel_ops.append(lambda g_=g_: S.add('dve', lambda e: e.max(out=top8[:, g_, :], in_=gsb[:, g_, :]), ['gsb'], ['top8']))
        sel_ops.append(lambda: S.add('dve', lambda e: e.tensor_tensor(gsb[:], gsb[:], top8[:, :, 2:3].to_broadcast([P, 32, 8]), ALU.is_ge), ['gsb', 'top8'], ['gsb']))
        sel_ops.append(lambda: S.add('dve', lambda e: e.tensor_scalar(selb[:], gsb[:], -1.0, 30000.0, ALU.add, ALU.mult), ['gsb'], ['selb']))
        for h in range(4):
            pT_ = ps[4][:].bitcast(BF16)
            for q8 in range(8):
                sel_ops.append(lambda h=h, q8=q8, pT_=pT_: S.add('pe', lambda e: e.transpose(pT_[0:8, q8 * P:(q8 + 1) * P], selb[:, h * 8 + q8, :], ident[:]),
                                                                 ['selb', 'ident'], [('ps', 4)]))
            sel_ops.append(lambda h=h, pT_=pT_: S.add('act', lambda e: e.copy(selT[0:8, h, :], pT_[0:8, :]), [('ps', 4)], [('selT', h)] + rtk))
        sel_state = [0]

        def selection(nops):
            for _ in range(nops):
                if sel_state[0] < len(sel_ops):
                    sel_ops[sel_state[0]]()
                    sel_state[0] += 1

        items = []
        for T in range(4):
            for h in range(4):
                for n in range(2 * T + 1):
                    items.append(('w', h, T, n, 0))
                    items.append(('w', h, T, n, 1))
                items.append(('n', h, T, 2 * T + 1, 0))
        NB = 3

        def warm(nd):
            for _ in range(nd):
                mm(ps[3][:], ident[:], triAB[:], True, True, ['ident', 'triAB'], [('ps', 3)])

        fill_units = []

        def add_fill_group(wbuf, wkey, kind, cbase):
            for j in range(4):
                for c in range(4):
                    fill_units.append((wbuf, wkey, kind, cbase, c, j))

        add_fill_group(wA, 'wA', 'qc', 0)
        nfirst = 10 ** 9
        fstate = {'u': 0, 'kc': 0, 'n': 0}

        def fill(nmm):
            for _ in range(nmm):
                if fstate['u'] >= len(fill_units):
                    return
                wbuf, wkey, kind, cbase, c, j = fill_units[fstate['u']]
                if fstate['u'] == nfirst and fstate['kc'] == 0:
                    dma('pool', wA[:], w_in_v[:, :, 1024:1536], [], ['wA'])
                b = 4 + (fstate['u'] % 2)
                kc = fstate['kc']
                mm(ps[b][:], wbuf[:, kc, c * P:(c + 1) * P], xT[:, kc, j * 512:(j + 1) * 512], kc == 0, kc == 7,
                   [wkey, ('xT', j)], [('ps', b)])
                fstate['kc'] += 1
                if fstate['kc'] == 8:
                    sl = slice(j * 512, (j + 1) * 512)
                    if kind == 'qc':
                        copy_evac(qc[:, c, sl], ps[b][:], [('ps', b)], [('qc', c, j), 'wC', 'wD'])
                    else:
                        k = fstate['n'] % 2
                        fstate['n'] += 1
                        act(thb[:, k, :], ps[b][:], AF.Tanh, [('ps', b)], [('thb', k)], scale=0.5)
                        S.add('dve', lambda e, k=k, b=b, cc=cbase + c, sl=sl: e.scalar_tensor_tensor(
                            cat[:, cc, sl], thb[:, k, :], 1.0, ps[b][:], ALU.add, ALU.mult),
                            [('thb', k), ('ps', b)], [('cat', cbase + c, j)])
                    fstate['kc'] = 0
                    fstate['u'] += 1

        dma('pool', wA[:], w_in_v[:, :, 4096:4608], [], ['wA'])
        dma('pool', wB[:], w_in_v[:, :, 4608:5120], [], ['wB'])

        def bankof(idx):
            return idx % 4

        def scores(idx):
            kind, h, T, n, half = items[idx]
            b = bankof(idx)
            if kind == 'w':
                kc_ = 2 * n + half
                qs = slice(T * 512, (T + 1) * 512)
                selw = (n < 2 * T and T >= 2)
                comp = (n == 2 * T)
                mm(ps[b][:], ka[:, h, kc_ * P:(kc_ + 1) * P], qa[:, h, qs], True, not (selw or comp),
                   [('ka', h, kc_ // 4), ('qa', h, T)], [('ps', b)])
                if selw:
                    mm(ps[b][:], esel[0:8, n, :], selT[0:8, h, (T - 2) * 512:(T - 1) * 512], False, True, ['esel', ('selT', h)], [('ps', b)])
                if comp:
                    mm(ps[b][:, 0:256], ident[:], triAB[:, half * 256:(half + 1) * 256], False, T < 2, ['ident', 'triAB'], [('ps', b)])
                    if T >= 2:
                        mm(ps[b][:, 256:512], esel[0:8, n, :], selT[0:8, h, (T - 2) * 512 + 256:(T - 1) * 512], False, True,
                           ['esel', ('selT', h)], [('ps', b)])
            else:
                jq = 2 * T + 1
                for hf in range(2):
                    kc_ = 2 * n + hf
                    hs = slice(hf * 256, (hf + 1) * 256)
                    mm(ps[b][:, hs], ka[:, h, kc_ * P:(kc_ + 1) * P], qa[:, h, jq * 256:(jq + 1) * 256], True, False,
                       [('ka', h, kc_ // 4), ('qa', h, T)], [('ps', b)])
                    mm(ps[b][:, hs], ident[:], triAB[:, hs], False, True, ['ident', 'triAB'], [('ps', b)])

        LA = 2
        fin = 0
        dstate = {'bank': 7}
        nsc = [0]

        def scores_upto(k):
            while nsc[0] <= min(k, len(items) - 1):
                scores(nsc[0])
                nsc[0] += 1

        scores_upto(2)
        for idx, (kind, h, T, n, half) in enumerate(items):
            if idx == 0:
                selection(8)
            elif idx >= 8:
                selection(4 if idx + 3 < 40 else 10 ** 6)
            scores_upto(idx + 3)
            if idx >= 38:
                fill(2 if idx < 58 else 1)
            b = bankof(idx)
            ke = idx % 4
            fi = fin % 2
            first = (n == 0 and half == 0)
            act(Ebuf[:, ke, :], ps[b][:], AF.Exp, [('ps', b)], [('E', ke)], scale=SC128)
            if first:
                dstate['bank'] = 7
            dbk = dstate['bank']
            if kind == 'w':
                kc_ = 2 * n + half
                mm(ps[6][:], va[:, kc_, h * P:(h + 1) * P], Ebuf[:, ke, :], first, False, [('va', kc_), ('E', ke)], [('ps', 6)])
                mm(ps[dbk][:], onesb[:], Ebuf[:, ke, :], first, False, ['onesb', ('E', ke)], [('ps', dbk)])
            else:
                for hf in range(2):
                    kc_ = 2 * n + hf
                    hs = slice(hf * 256, (hf + 1) * 256)
                    mm(ps[6][:, 256:512], va[:, kc_, h * P:(h + 1) * P], Ebuf[:, ke, hs], False, hf == 1, [('va', kc_), ('E', ke)], [('ps', 6)])
                    mm(ps[dbk][:, 256:512], onesb[:], Ebuf[:, ke, hs], False, hf == 1, ['onesb', ('E', ke)], [('ps', dbk)])
                fin += 1
                qs = slice(T * 512, (T + 1) * 512)
                dstc = cat[:, 8 + h, qs]
                ck = ('cat', 8 + h, T)
                act(rden2[:, fi, :], ps[dbk][:], AF.Ln, [('ps', dbk)], [('rden2', fi)])
                S.add('act', lambda e, fi=fi: e.copy(osb[:, fi, :], ps[6][:]), [('ps', 6)], [('osb', fi)])
                act(rden2[:, fi, :], rden2[:, fi, :], AF.Exp, [('rden2', fi)], [('rden2', fi)], scale=-1.0)
                S.add('dve', lambda e, fi=fi, dstc=dstc: e.tensor_tensor(t2c[:, fi, :], rden2[:, fi, :], dstc, ALU.mult),
                      [('rden2', fi), ck], [('t2c', fi)])
                S.add('dve', lambda e, fi=fi, dstc=dstc: e.tensor_tensor(dstc, osb[:, fi, :], t2c[:, fi, :], ALU.mult),
                      [('osb', fi), ('t2c', fi)], [ck])
        fill(10 ** 6)
        S.barrier()

        o = OFF_R
        wD = view(o, [P, 8, 512]); o += 8192
        wC = view(o, [P, 8, 512]); o += 8192
        o += 16384
        wA3 = view(OFF_XM + 24620, [P, 8, 512])
        memT = view(o, [P, 8, 256]); o += 4096
        kmT = view(o, [P, 4, 256]); o += 2048
        vm = view(o, [P, 2, 512]); o += 2048
        Eb = view(o, [P, 2, 2, 512]); o += 4096
        rden = view(o, [P, 2, 512], F32); o += 4096
        t2b = view(o, [P, 2, 512], F32); o += 4096
        assert o <= ARENA_BYTES
        dma('pool', wD[:], w_kv_v[:, :, 0:512], [], ['wD'])
        dma('pool', memT[:], memT_v, [], ['memT'])
        dma('pool', wA3[:], w_in_v[:, :, 0:512], [], ['wA3'])
        wV = view(OFF_XM + 16416, [P, 8, 512])
        dma('pool', wV[:], w_kv_v[:, :, 512:1024], [], ['wV'])
        wB3 = view(OFF_R + 57344, [P, 8, 512])
        dma('pool', wB3[:], w_in_v[:, :, 512:1024], [], ['wB3'])
        inproj_fm(wC, 'wC', ev_silu(12))
        for h in range(4):
            b = next_bank(0, 4)
            for kc in range(8):
                mm(ps[b][:, 0:256], wD[:, kc, h * P:(h + 1) * P], memT[:, kc, :], kc == 0, kc == 7, ['wD', 'memT'], [('ps', b)])
            copy_evac(kmT[:, h, :], ps[b][:, 0:256], [('ps', b)], ['kmT'])
        for mc in range(2):
            b = next_bank(0, 4)
            for kc in range(8):
                mm(ps[b][:], memT[:, kc, mc * P:(mc + 1) * P], wV[:, kc, :], kc == 0, kc == 7, ['wV', 'memT'], [('ps', b)])
            copy_evac(vm[:, mc, :], ps[b][:], [('ps', b)], ['vm'])
        hj = [(h, j) for h in range(4) for j in range(4)]

        def qk1(it):
            h, j = hj[it]
            bS = 0 + 2 * (it % 2)
            for mc in range(2):
                mm(ps[bS + mc][:], kmT[:, h, mc * P:(mc + 1) * P], qc[:, h, j * 512:(j + 1) * 512], True, True,
                   ['kmT', ('qc', h, j)], [('ps', bS + mc)])

        def fin1(it):
            h, j = hj[it]
            buf = it % 2
            bo = 4 + (it % 2)
            bd = 6
            act(rden[:, buf, :], ps[bd][:], AF.Ln, [('ps', bd)], [('rden', buf)])
            act(rden[:, buf, :], rden[:, buf, :], AF.Exp, [('rden', buf)], [('rden', buf)], scale=-1.0)
            dstc = cat[:, 12 + h, j * 512:(j + 1) * 512]
            S.add('dve', lambda e: e.tensor_tensor(t2b[:, buf, :], rden[:, buf, :], dstc, ALU.mult),
                  [('rden', buf), ('cat', 12 + h, j)], [('t2b', buf)])
            S.add('dve', lambda e: e.tensor_tensor(dstc, ps[bo][:], t2b[:, buf, :], ALU.mult),
                  [('ps', bo), ('t2b', buf)], [('cat', 12 + h, j)])

        xm1 = view(OFF_XM, [P, 8, 2051])

        def fill1(it):
            c, j = it % 4, it // 4
            for kc in range(8):
                mm(ps[7][:], wA3[:, kc, c * P:(c + 1) * P], xT[:, kc, j * 512:(j + 1) * 512], kc == 0, kc == 7,
                   ['wA3', ('xT', j)], [('ps', 7)])
            S.add('dve', lambda e: e.tensor_copy(xm1[:, c, 3 + j * 512:3 + (j + 1) * 512], ps[7][:]), [('ps', 7)], [('xm', c, j)])

        qk1(0)
        for it, (h, j) in enumerate(hj):
            if True:
                if it + 1 < len(hj):
                    qk1(it + 1)
                fill1(it)
                buf = it % 2
                bS = 0 + 2 * (it % 2)
                bo = 4 + (it % 2)
                bd = 6
                for mc in range(2):
                    act(Eb[:, buf, mc, :], ps[bS + mc][:], AF.Exp, [('ps', bS + mc)], [('Eb', buf, mc)], scale=SC128)
                for mc in range(2):
                    mm(ps[bo][:], vm[:, mc, h * P:(h + 1) * P], Eb[:, buf, mc, :], mc == 0, mc == 1,
                       ['vm', ('Eb', buf, mc)], [('ps', bo)])
                if it > 0:
                    fin1(it - 1)
                for mc in range(2):
                    mm(ps[bd][:], onesb[:], Eb[:, buf, mc, :], mc == 0, mc == 1, ['onesb', ('Eb', buf, mc)], [('ps', bd)])
        fin1(len(hj) - 1)
        S.barrier()

        o = OFF_R
        wA = view(o, [P, 8, 512])
        wB = view(o + 8192, [P, 8, 512])
        wqT32 = view(o, [P, 8, P], F32)
        wkT32 = view(o + 4096, [P, 8, P], F32)
        wvT32 = view(o + 8192, [P, 8, P], F32)
        gates_sb = view(o, [8, NT], F32)
        lf_sb = view(o + 8192, [8, NT], F32)
        qTh = view(o, [P, 2, NT])
        kTh = view(o + 8192, [P, 2, NT]); o += 16384
        khat_tm = view(o, [P, 16, 256])
        wg32 = view(o, [P, 24, 8], F32)
        gT = view(o + 768, [P, 16, 16], F32)
        Icont = view(o + 1792, [P, 64], F32)
        Acont = view(o + 2048, [P, 64], F32)
        tmp1 = view(o + 2304, [P, 64], F32)
        tmp2 = view(o + 2560, [P, 64], F32); o += 8192
        vaug = view(o, [P, 16, 258]); o += 8256
        tot32 = view(o, [P, 16, 257], F32)
        diagw = view(o, [P, 8, 4, P]); o += 16448
        wqb = view(o, [P, 8, P]); o += 2048
        wkb = view(o, [P, 8, P]); o += 2048
        wvb = view(o, [P, 8, P]); o += 2048
        G1b = view(o, [P, 8, 8]); o += 128
        G2b = view(o, [P, 8, 8]); o += 128
        w_s = view(o, [P, 64], F32); o += 256
        khat = view(o, [P, 64], F32); o += 256
        einv = view(o, [P, 64], F32); o += 256
        gdec = view(o, [P, 64], F32); o += 256
        C32 = view(o, [P, 2, 258], F32); o += 2064
        Cbf = view(o, [P, 2, 2, 258]); o += 2064
        Sm = view(o, [P, 2, P]); o += 512
        hnb = view(o, [P, 2, 4, 256]); o += 4096
        tg = view(o, [P, 2, 512], F32); o += 4096
        st6 = view(o, [P, 16, 6], F32); o += 384
        mv = view(o, [P, 16, 2], F32); o += 128
        sm = [view(o + 64 * q_, [P, 16], F32) for q_ in range(8)]; o += 512
        assert o <= ARENA_BYTES, o
        xm = view(OFF_XM, [P, 8, 2051])
        xc = view(OFF_XT, [P, 8, NT])
        wo = view(OFF_XM, [P, 16, 1024])

        wA3 = view(OFF_XM + 24620, [P, 8, 512])
        S.add('pool', lambda e: e.memset(xm[:, 0:6, 0:3], 0.0), [], ['xmpad'])
        wB3 = view(OFF_R + 57344, [P, 8, 512])
        dma('pool', wA[:], w_in_v[:, :, 1024:1536], [], ['wA'])
        dma('pool', wB[:], w_in_v[:, :, 1536:2048], [], ['wB'])
        dma('pool', wqb[:], wq_d, [], ['wqb'])
        dma('pool', wkb[:], wk_d, [], ['wkb'])
        dma('pool', wvb[:], wv_d, [], ['wvb'])

        def ev_xm(cbase):
            def f(c, j, pst, pkey):
                extra = ['wA3'] if cbase + c >= 6 else []
                copy_evac(xm[:, cbase + c, 3 + j * 512:3 + (j + 1) * 512], pst[:], [pkey], [('xm', cbase + c, j)] + extra)
            return f

        S.add('pool', lambda e: e.memset(xm[:, 6:8, 0:3], 0.0), [], ['xmpad2', 'wA3'])
        inproj_fm(wB3, 'wB3', ev_xm(4))
        inproj_fm(wA, 'wA', ev_silu(0))
        inproj_fm(wB, 'wB', ev_silu(4))
        S.barrier()

        dma('sp', wqT32[:], wqT_d, [], ['wT32'])
        dma('sp', wkT32[:], wkT_d, [], ['wT32'])
        dma('sp', wvT32[:], wvT_d, [], ['wT32'])
        dma('sp', wg32[:], wg_d, [], ['wg32'])
        for c in range(8):
            for tap in range(4):
                S.add('dve', lambda e, c=c, tap=tap: e.tensor_scalar(diagw[:, c, tap, :], ident[:], convw[:, c, tap:tap + 1], None, ALU.mult),
                      ['ident', 'convw'], [('dw', c)])
        for c in range(8):
            for j in range(4):
                b = next_bank(0, 4)
                rd = [('dw', c), ('xm', c, j), 'xmpad', 'xmpad2'] + ([('xm', c, j - 1)] if j > 0 else [])
                for tap in range(4):
                    mm(ps[b][:], diagw[:, c, tap, :], xm[:, c, j * 512 + tap:j * 512 + tap + 512], tap == 0, tap == 3, rd, [('ps', b)])
                act(xc[:, c, j * 512:(j + 1) * 512], ps[b][:], AF.Silu, [('ps', b), 'convb'], [('xc', c, j)], bias=convb[:, c:c + 1])

        for c in range(8):
            b = 4 + (c % 2)
            mm(ps[b][:, 0:8], wqT32[:, c, :], wg32[:, c, :], True, False, ['wT32', 'wg32'], [('ps', b)])
            mm(ps[b][:, 0:8], wkT32[:, c, :], wg32[:, 8 + c, :], False, True, ['wT32', 'wg32'], [('ps', b)])
            mm(ps[b][:, 8:16], wvT32[:, c, :], wg32[:, 16 + c, :], True, True, ['wT32', 'wg32'], [('ps', b)])
            copy_evac(G1b[:, c, :], ps[b][:, 0:8], [('ps', b)], ['G1b'])
            copy_evac(G2b[:, c, :], ps[b][:, 8:16], [('ps', b)], ['G2b'])
        for j in range(4):
            sl = slice(j * 512, (j + 1) * 512)
            for c in range(8):
                mm(ps[7][0:8, :], G1b[:, c, :], xc[:, c, sl], c == 0, False, ['G1b', ('xc', c, j)], [('ps', 7)])
                mm(ps[7][0:8, :], G2b[:, c, :], xm[:, c, 3 + j * 512:3 + (j + 1) * 512], False, c == 7, ['G2b', ('xm', c, j)], [('ps', 7)])
            act(gates_sb[0:8, sl], ps[7][0:8, :], AF.Identity, [('ps', 7), 'bg'], [('gates', j), 'wT32'], bias=bg[:, 0:1])
        gk = [('gates', j) for j in range(4)]
        act(lf_sb[0:8, :], gates_sb[0:8, :], AF.Exp, gk, ['lf', 'wT32'], scale=-1.0)
        act(lf_sb[0:8, :], lf_sb[0:8, :], AF.Ln, ['lf', 'one1'], ['lf'], bias=one1[0:8, 0:1])
        for i in range(16):
            b = 4 + (i % 3)
            S.add('pe', lambda e, i=i, b=b: e.transpose(ps[b][:, 0:8], gates_sb[0:8, i * P:(i + 1) * P], ident32[0:8, 0:8]), gk + ['ident32'], [('ps', b)])
            S.add('pe', lambda e, i=i, b=b: e.transpose(ps[b][:, 8:16], lf_sb[0:8, i * P:(i + 1) * P], ident32[0:8, 0:8]), ['lf', 'ident32'], [('ps', b)])
            copy_evac(gT[:, i, :], ps[b][:, 0:16], [('ps', b)], ['gT'])
        S.add('dve', lambda e: e.tensor_copy(Icont.rearrange("p (i h) -> p i h", h=4), gT[:, :, 0:4]), ['gT'], ['Icont'])
        S.add('dve', lambda e: e.tensor_copy(Acont.rearrange("p (i h) -> p i h", h=4), gT[:, :, 12:16]), ['gT'], ['Acont'])
        mm(ps[0][:, 0:64], U32[:], Acont, True, True, ['U32', 'Acont'], [('ps', 0)])
        mm(ps[1][:, 0:64], ones32[:], Acont, True, True, ['ones32', 'Acont'], [('ps', 1)])
        S.add('dve', lambda e: e.tensor_tensor(tmp1, Icont, ps[0][:, 0:64], ALU.add), ['Icont', ('ps', 0)], ['tmp1'])
        S.add('dve', lambda e: e.tensor_tensor(tmp2, tmp1, ps[1][:, 0:64], ALU.subtract), ['tmp1', ('ps', 1)], ['tmp2'])
        act(w_s, tmp1, AF.Exp, ['tmp1'], ['w_s'])
        act(khat, tmp2, AF.Exp, ['tmp2'], ['khat'])
        act(einv, ps[0][:, 0:64], AF.Exp, [('ps', 0)], ['einv'])
        act(gdec, ps[1][:, 0:64], AF.Exp, [('ps', 1)], ['gdec'], scale=-1.0)
        S.add('dve', lambda e: e.tensor_scalar(w_s, w_s, SC256, None, ALU.mult), ['w_s'], ['w_s'])
        S.add('dve', lambda e: e.tensor_scalar(khat, khat, SC256, None, ALU.mult), ['khat'], ['khat'])
        S.barrier()

        S.add('pool', lambda e: e.memset(vaug[:, :, 256:258], 1.0), [], ['vones'])
        allxm = [('xm', c, j) for c in range(8) for j in range(4)]
        absd, mcl, v2, rstd, nmr = sm[0:5]

        def setup(h):
            c0 = 2 * h
            for dc in range(2):
                c = c0 + dc
                for j in range(4):
                    sl = slice(j * 512, (j + 1) * 512)
                    b = next_bank(0, 4)
                    mm(ps[b][:], wqb[:, c, :], xc[:, c, sl], True, True, ['wqb', ('xc', c, j)], [('ps', b)])
                    copy_evac(qTh[:, dc, sl], ps[b][:], [('ps', b)], [('qTh', j)])
                    b = next_bank(0, 4)
                    mm(ps[b][:], wkb[:, c, :], xc[:, c, sl], True, True, ['wkb', ('xc', c, j)], [('ps', b)])
                    copy_evac(kTh[:, dc, sl], ps[b][:], [('ps', b)], [('kTh', j)])
            for i in range(16):
                col = i * 4 + h
                b = 4 + (i % 4)
                for dc in range(2):
                    mm(ps[b][:, dc * P:(dc + 1) * P], xc[:, c0 + dc, i * P:(i + 1) * P], wkb[:, c0 + dc, :], True, True,
                       ['wkb', ('xc', c0 + dc, i // 4)], [('ps', b)])
                for dc in range(2):
                    mm(ps[b][:, 256 + dc * P:256 + (dc + 1) * P], xm[:, c0 + dc, 3 + i * P:3 + (i + 1) * P], wvb[:, c0 + dc, :], True, True,
                       ['wvb', ('xm', c0 + dc, i // 4)], [('ps', b)])
                if i % 2 == 0:
                    S.add('dve', lambda e, i=i, b=b, col=col: e.tensor_scalar(khat_tm[:, i, :], ps[b][:, 0:256], khat[:, col:col + 1], None, ALU.mult),
                          [('ps', b), 'khat'], [('khat_tm', i)])
                    S.add('act', lambda e, i=i, b=b: e.copy(vaug[:, i, 0:256], ps[b][:, 256:512]), [('ps', b)], [('vaug', i)])
                else:
                    act(khat_tm[:, i, :], ps[b][:, 0:256], AF.Copy, [('ps', b), 'khat'], [('khat_tm', i)], scale=khat[:, col:col + 1])
                    S.add('dve', lambda e, i=i, b=b: e.tensor_copy(vaug[:, i, 0:256], ps[b][:, 256:512]), [('ps', b)], [('vaug', i)])
            if h == 3:
                for q4 in range(4):
                    dma('pool', wo[:, 4 * q4:4 * q4 + 4, :], w_out_v[:, 4 * q4:4 * q4 + 4, :], [], [('wo', q4)] + (allxm if q4 == 0 else []))
            for dc in range(2):
                c = c0 + dc
                S.add('dve', lambda e, c=c: e.tensor_scalar(xc[:, c, :], xc[:, c, :], skipv[:, c:c + 1], None, ALU.mult),
                      [('xc', c, j) for j in range(4)] + ['skipv'], [('xc', c, j) for j in range(4)])

        def p1_front(h, i):
            tsl = slice(i * P, (i + 1) * P)
            so = (i % 2) * P
            for dc in range(2):
                mm(ps[0][:, so:so + P], kTh[:, dc, tsl], qTh[:, dc, tsl], dc == 0, dc == 1, [('kTh', i // 4), ('qTh', i // 4)], [('ps', 0)])
            mm(ps[1][:], ident[:], triAB[:], True, True, ['ident', 'triAB'], [('ps', 1)])
            if i < 15:
                for dc in range(2):
                    bD = 4 + 2 * (i % 2) + dc
                    mm(ps[bD][:, 0:257], khat_tm[:, i, dc * P:(dc + 1) * P], vaug[:, i, 0:257], True, True,
                       [('khat_tm', i), ('vaug', i), 'vones'], [('ps', bD)])

        def p1_rest(h, i):
            col = i * 4 + h
            tsl = slice(i * P, (i + 1) * P)
            k = i % 2
            so = (i % 2) * P
            S.add('dve', lambda e: e.scalar_tensor_tensor(Sm[:, k, :], ps[0][:, so:so + P], w_s[:, col:col + 1], triT[:], ALU.mult, ALU.mult),
                  [('ps', 0), 'w_s', 'triT'], [('Sm', k)])
            if i < 15:
                for dc in range(2):
                    bD = 4 + 2 * (i % 2) + dc
                    if i == 0:
                        S.add('dve', lambda e, dc=dc, bD=bD: e.tensor_copy(C32[:, dc, 0:257], ps[bD][:, 0:257]), [('ps', bD)], [('C32', dc)])
                    else:
                        S.add('dve', lambda e, dc=dc, bD=bD: e.scalar_tensor_tensor(
                            C32[:, dc, 0:257], C32[:, dc, 0:257], gdec[:, col:col + 1], ps[bD][:, 0:257], ALU.mult, ALU.add),
                            [('C32', dc), ('ps', bD), 'gdec'], [('C32', dc)])
                    S.add('act', lambda e, dc=dc: e.copy(Cbf[:, i % 2, dc, 0:257], C32[:, dc, 0:257]), [('C32', dc)], [('Cbf', i % 2, dc)])
            mm(ps[2][:, 0:257], Sm[:, k, :], vaug[:, i, 0:257], True, i == 0, [('Sm', k), ('vaug', i), 'vones'], [('ps', 2)])
            if i > 0:
                par = (i - 1) % 2
                for dc in range(2):
                    mm(ps[2][:, 0:257], qTh[:, dc, tsl], Cbf[:, par, dc, 0:257], False, dc == 1,
                       [('qTh', i // 4), ('Cbf', par, dc)], [('ps', 2)])
            S.add('act', lambda e: e.copy(tot32[:, i, :], ps[2][:, 0:257]), [('ps', 2)], [('tot', i)])

        def p1_stats(i):
            S.add('dve', lambda e: e.bn_stats(st6[:, i, :], tot32[:, i, 0:256]), [('tot', i)], [('st6', i)])
            S.add('dve', lambda e: e.bn_aggr(mv[:, i, :], st6[:, i, :]), [('st6', i)], ['mv'])

        def batch(h):
            einv_h = einv.rearrange("p (i h) -> p i h", h=4)[:, :, h]
            dn_v = tot32[:, :, 256]
            mean_v = mv[:, :, 0]
            var_v = mv[:, :, 1]
            tk_ = [('tot', i) for i in range(16)]
            V = 'dve'
            S.add(V, lambda e: e.scalar_tensor_tensor(absd, dn_v, -1.0, dn_v, ALU.mult, ALU.max), tk_, ['absd'])
            S.add(V, lambda e: e.tensor_tensor(mcl, absd, einv_h, ALU.max), ['absd', 'einv'], ['mcl'])
            S.add(V, lambda e: e.tensor_tensor(v2, mcl, mcl, ALU.mult), ['mcl'], ['v2'])
            S.add(V, lambda e: e.scalar_tensor_tensor(v2, v2, LN_EPS, var_v, ALU.mult, ALU.add), ['v2', 'mv'], ['v2'])
            act(v2, v2, AF.Sqrt, ['v2'], ['v2'])
            S.add(V, lambda e: e.reciprocal(rstd, v2), ['v2'], ['rstd'])
            S.add(V, lambda e: e.scalar_tensor_tensor(nmr, mean_v, -1.0, rstd, ALU.mult, ALU.mult), ['mv', 'rstd'], ['nmr'])

        def pass2_group(h, g4):
            c0 = 2 * h
            kb = g4 % 2
            gsl = slice(g4 * 512, (g4 + 1) * 512)
            pTb = ps[3][:].bitcast(BF16)
            for ci in range(4):
                i = 4 * g4 + ci
                act(hnb[:, kb, ci, :], tot32[:, i, 0:256], AF.Identity, [('tot', i), 'rstd', 'nmr'], [('hnb', kb, ci)],
                    scale=rstd[:, i:i + 1], bias=nmr[:, i:i + 1])
                for ec in range(2):
                    sl_ = slice((ci * 2 + ec) * P, (ci * 2 + ec + 1) * P)
                    S.add('pe', lambda e, ci=ci, ec=ec, sl_=sl_: e.transpose(pTb[:, sl_], hnb[:, kb, ci, ec * P:(ec + 1) * P], ident[:]),
                          [('hnb', kb, ci), 'ident'], [('ps', 3)])
            for ec in range(2):
                c = c0 + ec
                pv = pTb.rearrange("p (ci ec t) -> p ci ec t", ci=4, ec=2)[:, :, ec, :]
                S.add('dve', lambda e, ec=ec, c=c, pv=pv: e.scalar_tensor_tensor(
                    tg[:, ec, :].rearrange("p (ci t) -> p ci t", ci=4), pv, normg[:, c:c + 1],
                    xc[:, c, gsl].rearrange("p (ci t) -> p ci t", ci=4), ALU.mult, ALU.add),
                    [('ps', 3), 'normg', ('xc', c, g4)], [('tg', ec)])
                S.add('pool', lambda e, ec=ec, c=c: e.tensor_tensor(cat[:, c, gsl], tg[:, ec, :], cat[:, c, gsl], ALU.mult),
                      [('tg', ec), ('cat', c, g4)], [('cat', c, g4)])

        for h in range(4):
            setup(h)
            p1_front(h, 0)
            for i in range(16):
                if h > 0 and i % 4 == 0:
                    pass2_group(h - 1, i // 4)
                if i + 1 < 16:
                    p1_front(h, i + 1)
                p1_rest(h, i)
                if i > 0:
                    p1_stats(i - 1)
            p1_stats(15)
            batch(h)
        for g4 in range(4):
            pass2_group(3, g4)
        S.barrier()

        if debug:
            dma('sp', dbg_cat, cat, [], ['dbg'])
        o = OFF_R
        lngb = view(o, [P, DM], F32); o += 4096
        lnbb = view(o, [P, DM], F32); o += 4096
        xrow = view(o, [P, 3, DM], F32); o += 12288
        pre = view(o, [P, 2, DM], F32); o += 8192
        yb = view(o, [P, 2, DM], F32); o += 8192
        st4 = view(o, [P, 2, 12], F32); o += 96
        mv4 = view(o, [P, 2, 2], F32); o += 16
        sc4 = view(o, [P, 2, 4], F32); o += 32
        dma('sp', lngb, lng_d.partition_broadcast(P), [], ['lngb'])
        dma('sp', lnbb, lnb_d.partition_broadcast(P), [], ['lnbb'])
        for i in range(2):
            dma('sp', xrow[:, i % 3, :], x_d[i * P:(i + 1) * P, :], [], [('xrow', i % 3)])
        for i in range(16):
            k = i % 2
            kx = i % 3
            tsl = slice(i * P, (i + 1) * P)
            if i + 2 < 16:
                dma('sp', xrow[:, (i + 2) % 3, :], x_d[(i + 2) * P:(i + 3) * P, :], [], [('xrow', (i + 2) % 3)])
            for half in range(2):
                b = next_bank(0, 4)
                hs = slice(half * 512, (half + 1) * 512)
                for kc in range(16):
                    mm(ps[b][:], cat[:, kc, tsl], wo[:, kc, hs], kc == 0, kc == 15, [('wo', kc // 4)], [('ps', b)])
                S.add('dve', lambda e, k=k, kx=kx, b=b, hs=hs: e.scalar_tensor_tensor(pre[:, k, hs], xrow[:, kx, hs], ALPHA, ps[b][:], ALU.mult, ALU.add),
                      [('xrow', kx), ('ps', b)], [('pre', k, half)])
                S.add('dve', lambda e, k=k, half=half, hs=hs: e.bn_stats(st4[:, k, 6 * half:6 * half + 6], pre[:, k, hs]), [('pre', k, half)], [('st4', k, half)])
            S.add('dve', lambda e, k=k: e.bn_aggr(mv4[:, k, :], st4[:, k, :]), [('st4', k, 0), ('st4', k, 1)], [('mv4', k)])
            S.add('dve', lambda e, k=k: e.tensor_scalar(sc4[:, k, 0:1], mv4[:, k, 1:2], LN_EPS, None, ALU.add), [('mv4', k)], [('sc4a', k)])
            act(sc4[:, k, 1:2], sc4[:, k, 0:1], AF.Sqrt, [('sc4a', k)], [('sc4b', k)])
            S.add('dve', lambda e, k=k: e.reciprocal(sc4[:, k, 2:3], sc4[:, k, 1:2]), [('sc4b', k)], [('sc4c', k)])
            S.add('dve', lambda e, k=k: e.scalar_tensor_tensor(sc4[:, k, 3:4], mv4[:, k, 0:1], -1.0, sc4[:, k, 2:3], ALU.mult, ALU.mult),
                  [('mv4', k), ('sc4c', k)], [('sc4d', k)])
            act(yb[:, k, :], pre[:, k, :], AF.Identity, [('pre', k, 0), ('pre', k, 1), ('sc4c', k), ('sc4d', k)], [('yb', k)],
                scale=sc4[:, k, 2:3], bias=sc4[:, k, 3:4])
            S.add('pool', lambda e, k=k: e.tensor_tensor(yb[:, k, :], yb[:, k, :], lngb, ALU.mult), [('yb', k), 'lngb'], [('yb', k)])
            S.add('pool', lambda e, k=k: e.tensor_tensor(yb[:, k, :], yb[:, k, :], lnbb, ALU.add), [('yb', k), 'lnbb'], [('yb', k)])
            dma('sp', y_d[tsl, :], yb[:, k, :], [('yb', k)], [('y', i)])

        S.barrier()
        S.flush()
    return nc


def _prep_shared(inp):
    f = np.float32
    sh = {}
    sh["w_in"] = np.ascontiguousarray(inp["w_in"], dtype=f)
    sh["w_kv"] = np.ascontiguousarray(inp["w_mem_kv"], dtype=f)
    sh["w_out"] = np.ascontiguousarray(inp["w_out"], dtype=f)
    sh["convw"] = np.ascontiguousarray(np.asarray(inp["mlstm_conv_w"], dtype=f).T.reshape(8, P, 4).transpose(1, 0, 2))
    sh["convb"] = np.ascontiguousarray(np.asarray(inp["mlstm_conv_b"], dtype=f).reshape(8, P).T)

    def bd(w):
        w = np.asarray(w, dtype=f)
        full = np.zeros((8, P, P), dtype=f)
        for g in range(256):
            c, r = divmod(g * 4, P)
            full[c, r:r + 4, r:r + 4] = w[g]
        return np.ascontiguousarray(full.transpose(1, 0, 2))

    sh["wq_bd"] = bd(inp["mlstm_wq"])
    sh["wk_bd"] = bd(inp["mlstm_wk"])
    sh["wv_bd"] = bd(inp["mlstm_wv"])

    def bdT(w):
        w = np.asarray(w, dtype=f)
        full = np.zeros((8, P, P), dtype=f)
        for g in range(256):
            c, r = divmod(g * 4, P)
            full[c, r:r + 4, r:r + 4] = w[g].T
        return np.ascontiguousarray(full.transpose(1, 0, 2))

    sh["wqT_bd"] = bdT(inp["mlstm_wq"])
    sh["wkT_bd"] = bdT(inp["mlstm_wk"])
    sh["wvT_bd"] = bdT(inp["mlstm_wv"])
    sh["wg"] = np.ascontiguousarray(np.asarray(inp["mlstm_w_gates"], dtype=f).reshape(24, P, 8).transpose(1, 0, 2))
    sh["bg"] = np.ascontiguousarray(np.asarray(inp["mlstm_b_gates"], dtype=f).reshape(8, 1))
    sh["normg"] = np.ascontiguousarray(np.asarray(inp["mlstm_norm_g"], dtype=f).reshape(8, P).T)
    sh["skip"] = np.ascontiguousarray(np.asarray(inp["mlstm_skip"], dtype=f).reshape(8, P).T)
    sh["ln_g"] = np.ascontiguousarray(np.asarray(inp["ln_g"], dtype=f).reshape(1, DM))
    sh["ln_b"] = np.ascontiguousarray(np.asarray(inp["ln_b"], dtype=f).reshape(1, DM))
    half = 64
    inv_freq = (np.float32(10000.0) ** (-(np.arange(half, dtype=np.float32) * np.float32(2.0) / np.float32(128)))).astype(np.float32)
    invf = (inv_freq.astype(np.float64) / (2.0 * np.pi)).astype(f)
    sh["invf"] = np.ascontiguousarray(np.concatenate([invf, invf]).reshape(P, 1))
    return sh


def _in_maps(inp, cores):
    sh = _prep_shared(inp)
    x = np.asarray(inp["x"], dtype=np.float32)
    mem = np.asarray(inp["mem"], dtype=np.float32)
    pos = np.asarray(inp["positions"], dtype=np.int32)
    maps = []
    for b in cores:
        m = dict(sh)
        m["x"] = np.ascontiguousarray(x[b])
        m["xT"] = np.ascontiguousarray(x[b].T)
        m["memT"] = np.ascontiguousarray(mem[b].T)
        m["pos"] = np.ascontiguousarray(pos[b].reshape(1, NT))
        maps.append(m)
    return maps


def kernel(**inputs):
    nc = build_nc(debug=False)
    maps = _in_maps(inputs, list(range(8)))
    res = run_bass_kernel_spmd(nc, maps, core_ids=list(range(8)))
    return np.stack([np.asarray(r["y"], dtype=np.float32) for r in res.results], axis=0)
```
